# Optimizing a Trainium2 kernel written in Bass

```python
import math
import jax, jax.numpy as jnp
from jax import lax
import numpy as np

D_MODEL = 1024
BATCH = 16
SEQ = 4096
DEPTH = 1
DEC_BATCH = 8
DEC_SEQ = 8192
PAST_LEN = 128

N_HEADS = 4
HEAD_DIM = 64
V_DIM = 2 * HEAD_DIM
ATTN_WIDTH = N_HEADS * V_DIM
QK_WIDTH = N_HEADS * 2 * HEAD_DIM
POOL_WINDOWS = (2, 4, 8, 16)
N_POOL_GROUPS = len(POOL_WINDOWS)
POOL_GROUP_DIM = 128
POOL_WIDTH = N_POOL_GROUPS * POOL_GROUP_DIM
MIX_WIDTH = ATTN_WIDTH + POOL_WIDTH
IN_WIDTH = 2 * QK_WIDTH + ATTN_WIDTH + POOL_WIDTH
D_FF = 2816
CONV_WIDTH = 3
ROPE_THETA = 10000.0
Q_BLOCK = 128
EPS = 1e-6

kernel_name = "hybrid_pool_diffattn_encoder"


def rms_norm(x, g):
    xf = x.astype(jnp.float32)
    y = xf * lax.rsqrt(jnp.mean(xf * xf, axis=-1, keepdims=True) + EPS)
    return (y * g.astype(jnp.float32)).astype(x.dtype)


def rope_tables(seq_len, dtype):
    inv = ROPE_THETA ** (-jnp.arange(0, HEAD_DIM, 2, dtype=jnp.float32) / HEAD_DIM)
    ang = jnp.arange(seq_len, dtype=jnp.float32)[:, None] * inv[None, :]
    ang = jnp.concatenate([ang, ang], axis=-1)
    return jnp.cos(ang).astype(dtype), jnp.sin(ang).astype(dtype)


def apply_rope(x, cos, sin):
    c = cos[None, :, None, None, :]
    s = sin[None, :, None, None, :]
    x1, x2 = jnp.split(x, 2, axis=-1)
    return x * c + jnp.concatenate([-x2, x1], axis=-1) * s


def diff_attention(q, k, v, lam):
    B, S = q.shape[0], q.shape[1]
    nb = S // Q_BLOCK
    scale = 1.0 / math.sqrt(HEAD_DIM)
    qb = jnp.moveaxis(q.reshape(B, nb, Q_BLOCK, N_HEADS, 2, HEAD_DIM), 1, 0)

    def one_block(qblk):
        s = jnp.einsum('bqhmd,bkhmd->bhmqk', qblk, k).astype(jnp.float32) * scale
        p = jax.nn.softmax(s, axis=-1)
        a = p[:, :, 0] - lam * p[:, :, 1]
        return jnp.einsum('bhqk,bkhe->bqhe', a.astype(v.dtype), v)

    o = lax.map(one_block, qb)
    return jnp.moveaxis(o, 0, 1).reshape(B, S, N_HEADS, V_DIM)


def multiscale_pool(p):
    B, S = p.shape[0], p.shape[1]
    pf = p.astype(jnp.float32)
    cs = jnp.concatenate([jnp.zeros((B, 1) + pf.shape[2:], jnp.float32),
                          jnp.cumsum(pf, axis=1)], axis=1)
    idx = jnp.arange(S)
    outs = []
    for g, w in enumerate(POOL_WINDOWS):
        lo = jnp.maximum(idx - w // 2, 0)
        hi = jnp.minimum(idx + w // 2 - 1, S - 1)
        csg = cs[:, :, g]
        tot = csg[:, hi + 1] - csg[:, lo]
        cnt = (hi - lo + 1).astype(jnp.float32)[None, :, None]
        outs.append(tot / cnt - pf[:, :, g])
    return jnp.stack(outs, axis=2).astype(p.dtype)


def depthwise_conv_centred(h, w, b):
    hp = jnp.pad(h, ((0, 0), (1, 1), (0, 0)))
    S = h.shape[1]
    return hp[:, 0:S] * w[0] + hp[:, 1:S + 1] * w[1] + hp[:, 2:S + 2] * w[2] + b


def encoder_layer(x, layer_idx, norm1_g, w_in, q_norm_g, k_norm_g, lambda_q1,
                  lambda_k1, lambda_q2, lambda_k2, subln_g, w_pool, pool_scale,
                  w_out, norm2_g, w_up, conv_w, conv_b, w_down):
    B, S, _ = x.shape
    lambda_init = 0.8 - 0.6 * math.exp(-0.3 * layer_idx)
    h = rms_norm(x, norm1_g)
    z = jnp.einsum('bsd,de->bse', h, w_in)
    zq, zk, zv, zp = jnp.split(
        z, [QK_WIDTH, 2 * QK_WIDTH, 2 * QK_WIDTH + ATTN_WIDTH], axis=-1)
    q = rms_norm(zq.reshape(B, S, N_HEADS, 2, HEAD_DIM), q_norm_g)
    k = rms_norm(zk.reshape(B, S, N_HEADS, 2, HEAD_DIM), k_norm_g)
    cos, sin = rope_tables(S, x.dtype)
    q = apply_rope(q, cos, sin)
    k = apply_rope(k, cos, sin)
    v = zv.reshape(B, S, N_HEADS, V_DIM)
    lam = (jnp.exp(jnp.sum(lambda_q1.astype(jnp.float32) * lambda_k1.astype(jnp.float32)))
           - jnp.exp(jnp.sum(lambda_q2.astype(jnp.float32) * lambda_k2.astype(jnp.float32)))
           + lambda_init)
    o = diff_attention(q, k, v, lam)
    o = rms_norm(o, subln_g) * (1.0 - lambda_init)
    o_attn = o.reshape(B, S, ATTN_WIDTH)
    pg = multiscale_pool(zp.reshape(B, S, N_POOL_GROUPS, POOL_GROUP_DIM))
    pg = jnp.einsum('bsgc,gce->bsge', pg, w_pool).reshape(B, S, POOL_WIDTH)
    o_pool = pg * pool_scale
    mixed = jnp.concatenate([o_attn, o_pool], axis=-1)
    x = x + jnp.einsum('bse,ed->bsd', mixed, w_out)
    h2 = rms_norm(x, norm2_g)
    u = jnp.einsum('bsd,df->bsf', h2, w_up)
    u = depthwise_conv_centred(u, conv_w, conv_b)
    gate, val = jnp.split(u, 2, axis=-1)
    x = x + jnp.einsum('bsf,fd->bsd', jax.nn.silu(gate) * val, w_down)
    return x


def setup_inputs(seed: int = 0) -> dict:
    key = jax.random.key(seed)
    ks = jax.random.split(key, 20)
    n = lambda k, shape, s: jax.random.normal(k, shape, jnp.float32) * s
    L = DEPTH
    return {
        "x_prompt": n(ks[0], (BATCH, SEQ, D_MODEL), 1.0),
        "x_sample": n(ks[1], (DEC_BATCH, DEC_SEQ, D_MODEL), 1.0),
        "norm1_g": 1.0 + n(ks[2], (L, D_MODEL), 0.02),
        "w_in": n(ks[3], (L, D_MODEL, IN_WIDTH), D_MODEL ** -0.5),
        "q_norm_g": 1.0 + n(ks[4], (L, HEAD_DIM), 0.02),
        "k_norm_g": 1.0 + n(ks[5], (L, HEAD_DIM), 0.02),
        "lambda_q1": n(ks[6], (L, HEAD_DIM), 0.1),
        "lambda_k1": n(ks[7], (L, HEAD_DIM), 0.1),
        "lambda_q2": n(ks[8], (L, HEAD_DIM), 0.1),
        "lambda_k2": n(ks[9], (L, HEAD_DIM), 0.1),
        "subln_g": 1.0 + n(ks[10], (L, V_DIM), 0.02),
        "w_pool": n(ks[11], (L, N_POOL_GROUPS, POOL_GROUP_DIM, POOL_GROUP_DIM), POOL_GROUP_DIM ** -0.5),
        "pool_scale": 1.0 + n(ks[12], (L, POOL_WIDTH), 0.1),
        "w_out": n(ks[13], (L, MIX_WIDTH, D_MODEL), MIX_WIDTH ** -0.5),
        "norm2_g": 1.0 + n(ks[14], (L, D_MODEL), 0.02),
        "w_up": n(ks[15], (L, D_MODEL, 2 * D_FF), D_MODEL ** -0.5),
        "conv_w": n(ks[16], (L, CONV_WIDTH, 2 * D_FF), CONV_WIDTH ** -0.5),
        "conv_b": n(ks[17], (L, 2 * D_FF), 0.01),
        "w_down": n(ks[18], (L, D_FF, D_MODEL), D_FF ** -0.5),
    }


def reference(x_prompt, x_sample, norm1_g, w_in, q_norm_g, k_norm_g, lambda_q1,
              lambda_k1, lambda_q2, lambda_k2, subln_g, w_pool, pool_scale,
              w_out, norm2_g, w_up, conv_w, conv_b, w_down):
    yp = x_prompt
    ys = x_sample
    for l in range(DEPTH):
        params = (norm1_g[l], w_in[l], q_norm_g[l], k_norm_g[l], lambda_q1[l],
                  lambda_k1[l], lambda_q2[l], lambda_k2[l], subln_g[l], w_pool[l],
                  pool_scale[l], w_out[l], norm2_g[l], w_up[l], conv_w[l],
                  conv_b[l], w_down[l])
        yp = encoder_layer(yp, l, *params)
        ys = encoder_layer(ys, l, *params)
    return (yp, ys)
```

```python
from contextlib import ExitStack
import math
import numpy as np
import concourse.bass as bass
import concourse.mybir as mybir
from concourse.bass_utils import run_bass_kernel_spmd

F32 = mybir.dt.float32
BF16 = mybir.dt.bfloat16
I32 = mybir.dt.int32
ALU = mybir.AluOpType
AF = mybir.ActivationFunctionType
AX = mybir.AxisListType

D = 1024
DFF = 2816
NFC = DFF // 128
EPS = 1e-6
LAMBDA_INIT = 0.8 - 0.6 * math.exp(-0.3 * 0)
POOL_W = (2, 4, 8, 16)
ENGS = ("pe", "act", "dve", "pool", "sp")


class Buf:
    __slots__ = ("name", "w", "r", "const", "sem", "dcount", "psum")

    def __init__(self, name="", psum=False):
        self.name = name
        self.psum = psum
        self.w = None
        self.r = []
        self.const = False
        self.sem = None
        self.dcount = 0


class Op:
    __slots__ = ("eng", "idx", "fn", "deps", "signal", "val", "dslot", "dval", "waits")


class Sched:
    def __init__(self):
        self.ops = {e: [] for e in ENGS}
        self.dma_slots = []
        self.dma_ops = []
        self.fence_deps = None
        self.fence_pending = set()
        self.muted = False
        self.limit = None

    def fence(self):
        deps = [l[-1] for l in self.ops.values() if l] + self.dma_ops
        self.dma_ops = []
        self.fence_deps = deps
        self.fence_pending = set(ENGS)

    def op(self, eng, fn, reads=(), writes=(), dma=None):
        if self.muted:
            return None
        if self.limit is not None:
            self.limit -= 1
            if self.limit < 0:
                return None
        o = Op()
        o.eng = eng
        o.fn = fn
        o.signal = False
        o.val = None
        o.dslot = dma
        o.dval = None
        deps = set()
        if eng in self.fence_pending:
            deps.update(self.fence_deps)
            self.fence_pending.discard(eng)
        for b in reads:
            if b.w is not None:
                deps.add(b.w)
            if b.psum:
                deps.update(r for r in b.r if r.eng != eng)
        for b in writes:
            if b.w is not None:
                deps.add(b.w)
            deps.update(b.r)
        o.deps = deps
        for b in reads:
            if not b.const:
                b.r.append(o)
        for b in writes:
            b.w = o
            b.r = []
        o.idx = len(self.ops[eng])
        self.ops[eng].append(o)
        if dma is not None:
            if dma.sem is None:
                dma.sem = True
                self.dma_slots.append(dma)
            dma.dcount += 16
            o.dval = dma.dcount
            self.dma_ops.append(o)
        return o

    def emit(self, nc, final_waits=()):
        for e in ENGS:
            waited = {}
            for o in self.ops[e]:
                best = {}
                for d in o.deps:
                    if d.dslot is not None:
                        key = ("d", id(d.dslot))
                        v = d.dval
                    else:
                        if d.eng == "pe" and e == "pe":
                            continue
                        key = d.eng
                        v = d.idx
                    if key not in best or best[key][0] < v:
                        best[key] = (v, d)
                ws = []
                for key, (v, d) in best.items():
                    if waited.get(key, -1) >= v:
                        continue
                    waited[key] = v
                    if d.dslot is None:
                        d.signal = True
                    ws.append(d)
                o.waits = ws
                o.deps = None
        with ExitStack() as st:
            sems = {e: st.enter_context(nc.semaphore("s_" + e)) for e in ENGS}
            for i, sl in enumerate(self.dma_slots):
                sl.sem = st.enter_context(nc.semaphore("d%d" % i))
            for e in ENGS:
                c = 0
                for o in self.ops[e]:
                    if o.dslot is None and o.signal:
                        c += 1
                        o.val = c
                assert c < 65000, (e, c)
            block = st.enter_context(nc.Block())
            handles = {"pe": block.tensor, "act": block.scalar, "dve": block.vector,
                       "pool": block.gpsimd, "sp": block.sync}
            for e in ENGS:
                ops = self.ops[e]

                def body(eng, ops=ops, e=e):
                    for o in ops:
                        for d in o.waits:
                            if d.dslot is not None:
                                eng.wait_ge(d.dslot.sem, d.dval)
                            else:
                                eng.wait_ge(sems[d.eng], d.val)
                        ins = o.fn(eng)
                        if o.dslot is not None:
                            ins.then_inc(o.dslot.sem, 16)
                        elif o.signal:
                            ins.then_inc(sems[e], 1)
                    if e == "sp":
                        for b in self.dma_slots:
                            eng.wait_ge(b.sem, b.dcount)

                handles[e](body)


def bands_const():
    S = 384
    out = np.zeros((4, 5, 128, 128), np.float32)
    idx = np.arange(S)
    for g, w in enumerate(POOL_W):
        lo = np.maximum(idx - w // 2, 0)
        hi = np.minimum(idx + w // 2 - 1, S - 1)
        cnt = (hi - lo + 1).astype(np.float64)
        M = np.zeros((S, S))
        for i in range(S):
            M[lo[i]:hi[i] + 1, i] = 1.0 / cnt[i]
            M[i, i] -= 1.0
        out[g, 0] = M[0:128, 128:256]
        out[g, 1] = M[128:256, 128:256]
        out[g, 2] = M[256:384, 128:256]
        out[g, 3] = M[0:128, 0:128]
        out[g, 4] = M[256:384, 256:384]
    return out.reshape(20, 128, 128)


def build(seqs, debug=False):
    nc = bass.Bass("TRN2", target_bir_lowering=False)
    TOK = sum(seqs)
    offs = [sum(seqs[:i]) for i in range(len(seqs))]
    SMAX = max(seqs)
    NTMAX = SMAX // 128

    def din(name, shape, dt=F32):
        return nc.dram_tensor(name, list(shape), dt, kind="ExternalInput").ap()

    x_d = din("x", [TOK, D])
    y_d = nc.dram_tensor("y", [TOK, D], F32, kind="ExternalOutput").ap()
    norm1_g = din("norm1_g", [D]); w_in = din("w_in", [D, 2048])
    qg_d = din("q_norm_g", [64]); kg_d = din("k_norm_g", [64])
    lq1 = din("lambda_q1", [64]); lk1 = din("lambda_k1", [64])
    lq2 = din("lambda_q2", [64]); lk2 = din("lambda_k2", [64])
    subln_g = din("subln_g", [128]); w_pool = din("w_pool", [4, 128, 128])
    pool_scale = din("pool_scale", [512]); w_out = din("w_out", [D, D])
    norm2_g = din("norm2_g", [D]); w_up = din("w_up", [D, 2 * DFF])
    conv_w = din("conv_w", [3, 2 * DFF]); conv_b = din("conv_b", [2 * DFF])
    w_down = din("w_down", [DFF, D])
    bands_d = din("bands", [20, 128, 128]); invf_d = din("invf", [32])

    def scratch(name, shape):
        return nc.dram_tensor(name, list(shape), BF16, kind="Internal").ap()

    QT_d = scratch("QT", [4, 128, TOK]); KT_d = scratch("KT", [4, 128, TOK])
    V_d = scratch("Vs", [4, 128, TOK // 128, 129])
    MT_d = scratch("MT", [8, 128, TOK])

    S = Sched()
    op = S.op
    dbuf = {}

    def DB(name, blk):
        k = (name, blk)
        if k not in dbuf:
            dbuf[k] = Buf("%s%d" % (name, blk))
        return dbuf[k]

    out_bufs = []

    with ExitStack() as top:
        def T(st, name, shape, dt):
            return st.enter_context(nc.sbuf_tensor(name, list(shape), dt))

        def P(st, name, shape, dt=F32):
            return st.enter_context(nc.psum_tensor(name, list(shape), dt))

        ident = T(top, "ident", [128, 128], BF16); identf = T(top, "identf", [128, 128], F32)
        neghalf = T(top, "neghalf", [128, 16], F32)
        neglam = T(top, "neglam", [128, 1], F32)
        negB = T(top, "negB", [128, 1], F32); gmx = T(top, "gmx", [128, 2], F32); b_negB = Buf("negB")
        lamt = T(top, "lamt", [128, 4, 64], F32); lams = T(top, "lams", [128, 2], F32)
        b_ident = Buf("ident"); b_nh = Buf("nh"); b_lam = Buf("lam")
        op("pool", lambda e: e.memset(identf[:], 1.0), writes=[b_ident])
        op("pool", lambda e: e.affine_select(out=identf[:], in_=identf[:], pattern=[[-1, 128]],
                                             compare_op=ALU.is_equal, fill=0.0, base=0,
                                             channel_multiplier=1), reads=[b_ident], writes=[b_ident])
        op("pool", lambda e: e.tensor_copy(out=ident[:], in_=identf[:]), reads=[b_ident], writes=[b_ident])
        op("pool", lambda e: e.memset(neghalf[:], -0.5), writes=[b_nh])
        for i, v in enumerate((lq1, lk1, lq2, lk2)):
            op("sp", lambda e, i=i, v=v: e.dma_start(out=lamt[:, i, :], in_=v.partition_broadcast(128)),
               writes=[b_lam], dma=b_lam)
        op("dve", lambda e: e.tensor_tensor(out=lamt[:, 0, :], in0=lamt[:, 0, :], in1=lamt[:, 1, :], op=ALU.mult),
           reads=[b_lam], writes=[b_lam])
        op("dve", lambda e: e.tensor_tensor(out=lamt[:, 2, :], in0=lamt[:, 2, :], in1=lamt[:, 3, :], op=ALU.mult),
           reads=[b_lam], writes=[b_lam])
        op("dve", lambda e: e.tensor_reduce(out=lams[:, 0:1], in_=lamt[:, 0, :], axis=AX.X, op=ALU.add),
           reads=[b_lam], writes=[b_lam])
        op("dve", lambda e: e.tensor_reduce(out=lams[:, 1:2], in_=lamt[:, 2, :], axis=AX.X, op=ALU.add),
           reads=[b_lam], writes=[b_lam])
        op("act", lambda e: e.activation(out=lams[:], in_=lams[:], func=AF.Exp), reads=[b_lam], writes=[b_lam])
        op("dve", lambda e: e.scalar_tensor_tensor(out=neglam[:], in0=lams[:, 1:2], scalar=-LAMBDA_INIT,
                                                   in1=lams[:, 0:1], op0=ALU.add, op1=ALU.subtract),
           reads=[b_lam], writes=[b_lam])
        b_ident.const = True; b_nh.const = True; b_lam.const = True

        def rsqrt_ops(eng_tag, dst, src, n, scale, rb, wb):
            op("dve", lambda e: e.tensor_scalar(out=dst, in0=src, scalar1=scale, scalar2=EPS,
                                                op0=ALU.mult, op1=ALU.add), reads=rb, writes=wb)
            op("pool", lambda e: e.tensor_tensor(out=dst, in0=dst, in1=neghalf[:, 0:n], op=ALU.pow),
               reads=wb + [b_nh], writes=wb)

        with ExitStack() as p1:
            if _STOP < 0.2:
                S.muted = True
            w_in_bf = T(p1, "w_in_bf", [128, 8, 2048], BF16)
            g1 = T(p1, "g1", [128, 8], F32)
            Cq = T(p1, "Cq", [128, NTMAX, 64], F32); Sq = T(p1, "Sq", [128, NTMAX, 64], F32)
            Ck = T(p1, "Ck", [128, NTMAX, 64], F32); Sk = T(p1, "Sk", [128, NTMAX, 64], F32)
            wpl = T(p1, "wpl", [128, 4, 128], BF16)
            psc = T(p1, "psc", [128, 4], F32)
            band = T(p1, "band", [128, 20, 128], BF16)
            pw1 = ExitStack()
            wst = [T(pw1, "wst%d" % i, [128, 2048], F32) for i in range(2)]
            wplf = T(pw1, "wplf", [128, 4, 128], F32)
            bandf = T(pw1, "bandf", [128, 20, 128], F32)
            b_win = Buf("w_in"); b_wst = [Buf("wst0"), Buf("wst1")]; b_g1 = Buf("g1")
            b_tab = Buf("tab"); b_wpl = Buf("wpl"); b_band = Buf("band")
            op("sp", lambda e: e.dma_start(out=g1[:], in_=norm1_g.rearrange("(k p) -> p k", p=128), allow_slow_non_contiguous=True),
               writes=[b_g1], dma=b_g1)
            for k in range(8):
                s_ = k % 2
                op("sp", lambda e, k=k, s_=s_: e.dma_start(out=wst[s_][:], in_=w_in[k * 128:(k + 1) * 128, :]),
                   writes=[b_wst[s_]], dma=b_wst[s_])
                op("dve", lambda e, k=k, s_=s_: e.tensor_scalar(out=w_in_bf[:, k, :], in0=wst[s_][:],
                                                                scalar1=g1[:, k:k + 1], scalar2=None, op0=ALU.mult),
                   reads=[b_wst[s_], b_g1], writes=[b_win])
            op("sp", lambda e: e.dma_start(out=wplf[:], in_=w_pool.rearrange("g c e -> c g e")),
               writes=[b_wpl], dma=b_wpl)
            op("sp", lambda e: e.dma_start(out=psc[:], in_=pool_scale.rearrange("(g e) -> e g", e=128), allow_slow_non_contiguous=True),
               writes=[b_wpl], dma=b_wpl)
            op("pool", lambda e: e.tensor_copy(out=wpl[:], in_=wplf[:]), reads=[b_wpl], writes=[b_wpl])
            op("sp", lambda e: e.dma_start(out=bandf[:], in_=bands_d.rearrange("n j i -> j n i")),
               writes=[b_band], dma=b_band)
            op("pool", lambda e: e.tensor_copy(out=band[:], in_=bandf[:]), reads=[b_band], writes=[b_band])

            if _STOP < 0.4:
                S.muted = True
            with ExitStack() as pt:
                S.limit = _LIMIT
                ang = T(pt, "ang", [128, NTMAX, 32], F32); kf = T(pt, "kf", [128, NTMAX, 32], F32)
                ki = T(pt, "ki", [128, NTMAX, 32], I32); mm = T(pt, "mm", [128, NTMAX, 32], F32)
                rc = T(pt, "rc", [128, NTMAX, 32], F32)
                sn = T(pt, "sn", [128, NTMAX, 32], F32); cs = T(pt, "cs", [128, NTMAX, 32], F32)
                invb = T(pt, "invb", [128, 32], F32); gq = T(pt, "gq", [128, 64], F32); gk = T(pt, "gk", [128, 64], F32)
                b_a = Buf("ang")
                C1 = 6.28125
                C2 = 2 * math.pi - 6.28125
                PI = math.pi
                op("sp", lambda e: e.dma_start(out=invb[:], in_=invf_d.partition_broadcast(128)), writes=[b_a], dma=b_a)
                op("sp", lambda e: e.dma_start(out=gq[:], in_=qg_d.partition_broadcast(128)), writes=[b_a], dma=b_a)
                op("sp", lambda e: e.dma_start(out=gk[:], in_=kg_d.partition_broadcast(128)), writes=[b_a], dma=b_a)
                op("pool", lambda e: e.iota(ki[:], pattern=[[128, NTMAX], [0, 32]], base=0, channel_multiplier=1),
                   reads=[b_a], writes=[b_a])
                op("pool", lambda e: e.tensor_copy(out=ang[:], in_=ki[:]), reads=[b_a], writes=[b_a])
                bc32 = lambda a: a.unsqueeze(1).to_broadcast([128, NTMAX, 32])
                dv = lambda f: op("dve", f, reads=[b_a], writes=[b_a])
                dv(lambda e: e.tensor_tensor(out=ang[:], in0=ang[:], in1=bc32(invb[:]), op=ALU.mult))
                dv(lambda e: e.tensor_scalar(out=kf[:], in0=ang[:], scalar1=1.0 / (2 * PI), scalar2=None, op0=ALU.mult))
                dv(lambda e: e.tensor_copy(out=ki[:], in_=kf[:]))
                dv(lambda e: e.tensor_copy(out=kf[:], in_=ki[:]))
                dv(lambda e: e.scalar_tensor_tensor(out=ang[:], in0=kf[:], scalar=-C1, in1=ang[:], op0=ALU.mult, op1=ALU.add))
                dv(lambda e: e.scalar_tensor_tensor(out=ang[:], in0=kf[:], scalar=-C2, in1=ang[:], op0=ALU.mult, op1=ALU.add))

                def wrap(t):
                    dv(lambda e: e.tensor_scalar(out=mm[:], in0=t[:], scalar1=PI, scalar2=-2 * PI, op0=ALU.is_gt, op1=ALU.mult))
                    dv(lambda e: e.tensor_tensor(out=t[:], in0=t[:], in1=mm[:], op=ALU.add))
                    dv(lambda e: e.tensor_scalar(out=mm[:], in0=t[:], scalar1=-PI, scalar2=2 * PI, op0=ALU.is_lt, op1=ALU.mult))
                    dv(lambda e: e.tensor_tensor(out=t[:], in0=t[:], in1=mm[:], op=ALU.add))
                wrap(ang)
                dv(lambda e: e.tensor_scalar(out=rc[:], in0=ang[:], scalar1=PI / 2, scalar2=None, op0=ALU.add))
                wrap(rc)
                op("act", lambda e: e.activation(out=sn[:], in_=ang[:], func=AF.Sin), reads=[b_a], writes=[b_a])
                op("act", lambda e: e.activation(out=cs[:], in_=rc[:], func=AF.Sin), reads=[b_a], writes=[b_a])
                op("dve", lambda e: e.tensor_reduce(out=gmx[:, 0:1], in_=gq[:], axis=AX.X, op=ALU.max, apply_absolute_value=True),
                   reads=[b_a], writes=[b_negB])
                op("dve", lambda e: e.tensor_reduce(out=gmx[:, 1:2], in_=gk[:], axis=AX.X, op=ALU.max, apply_absolute_value=True),
                   reads=[b_a, b_negB], writes=[b_negB])
                op("dve", lambda e: e.scalar_tensor_tensor(out=negB[:], in0=gmx[:, 0:1], scalar=-8.0, in1=gmx[:, 1:2],
                                                           op0=ALU.mult, op1=ALU.mult), reads=[b_negB], writes=[b_negB])
                for (Ct, St, g, sc) in ((Cq, Sq, gq, 0.125), (Ck, Sk, gk, 1.0)):
                    for hf in range(2):
                        lo, hi = hf * 32, hf * 32 + 32
                        olo, ohi = (1 - hf) * 32, (1 - hf) * 32 + 32
                        sgn = -sc if hf == 0 else sc
                        op("dve", lambda e, Ct=Ct, g=g, lo=lo, hi=hi, sc=sc: e.scalar_tensor_tensor(
                            out=Ct[:, :, lo:hi], in0=cs[:], scalar=sc, in1=bc32(g[:, lo:hi]), op0=ALU.mult, op1=ALU.mult),
                           reads=[b_a], writes=[b_tab])
                        op("dve", lambda e, St=St, g=g, lo=lo, hi=hi, olo=olo, ohi=ohi, sgn=sgn: e.scalar_tensor_tensor(
                            out=St[:, :, lo:hi], in0=sn[:], scalar=sgn, in1=bc32(g[:, olo:ohi]), op0=ALU.mult, op1=ALU.mult),
                           reads=[b_a], writes=[b_tab])
                S.fence()
            pw1.close()
            b_win.const = True; b_tab.const = True; b_wpl.const = True; b_band.const = True; b_negB.const = True

            if _STOP < 0.6:
                S.muted = True
            NX = 4
            xs = [T(p1, "xs%d" % i, [128, D], F32) for i in range(NX)]; b_xs = [Buf() for _ in range(NX)]
            junk = T(p1, "junk", [128, D], BF16); b_junk = Buf()
            ss1 = [T(p1, "ss1_%d" % i, [128, 1], F32) for i in range(3)]; b_ss1 = [Buf(), Buf(), Buf()]
            hb = [T(p1, "hb%d" % i, [128, D], BF16) for i in range(3)]; b_hb = [Buf(), Buf(), Buf()]
            hT = [T(p1, "hT%d" % i, [128, 8, 128], BF16) for i in range(2)]; b_hT = [Buf(), Buf()]
            sq = [[T(p1, "sq%d_%d" % (j, i), [128, 8, 64], F32) for i in range(2)] for j in range(2)]
            b_sq = [[Buf(), Buf()], [Buf(), Buf()]]
            zsb = [[T(p1, "zsb%d_%d" % (j, i), [128, 512], F32) for i in range(2)] for j in range(2)]
            b_zsb = [[Buf(), Buf()], [Buf(), Buf()]]
            ta = [[T(p1, "ta%d_%d" % (j, i), [128, 8, 64], F32) for i in range(2)] for j in range(2)]
            tb = [[T(p1, "tb%d_%d" % (j, i), [128, 8, 64], F32) for i in range(2)] for j in range(2)]
            rs = [T(p1, "rs_%d" % i, [128, 2, 8], F32) for i in range(2)]
            b_ta = [[Buf(), Buf()], [Buf(), Buf()]]; b_tb = [[Buf(), Buf()], [Buf(), Buf()]]; b_rs = [Buf(), Buf()]
            qkb = [[T(p1, "qkb%d_%d" % (i, j), [128, 512], BF16) for j in range(2)] for i in range(2)]
            b_qkb = [[Buf(), Buf()], [Buf(), Buf()]]
            NPB = 4
            pb = [T(p1, "pb%d" % i, [128, 512], BF16) for i in range(NPB)]; b_pb = [Buf() for _ in range(NPB)]
            pgT = T(p1, "pgT", [128, 4, 128], BF16); b_pgT = Buf()
            qst = [[T(p1, "qst%d_%d" % (i, j), [128, 4, 512], BF16) for j in range(2)] for i in range(2)]
            b_qst = [[Buf(), Buf()], [Buf(), Buf()]]
            vst = [T(p1, "vst%d" % i, [128, 4, 4, 129], BF16) for i in range(2)]; b_vst = [Buf() for _ in range(2)]
            for i_ in range(2):
                op("pool", lambda e, i_=i_: e.memset(vst[i_][:, :, :, 128:129], 1.0), writes=[b_vst[i_]])
            ost = [T(p1, "ost%d" % i, [128, 4, 512], BF16) for i in range(2)]; b_ost = [Buf(), Buf()]
            with ExitStack() as pp:
                zps = [P(pp, "z%d" % i, [128, 512]) for i in range(4)]; b_z = [Buf(psum=True) for _ in range(4)]
                pTp = P(pp, "pTp", [128, 8, 128], BF16); b_pTp = Buf(psum=True)
                qkTp = P(pp, "qkTp", [128, 2, 4, 128], BF16); b_qkTp = Buf(psum=True)
                plp = P(pp, "plp", [128, 4, 128]); b_plp = Buf(psum=True)
                opp = P(pp, "opp", [128, 4, 128]); b_opp = Buf(psum=True)

                tiles = []
                for si, Sl in enumerate(seqs):
                    for t in range(Sl // 128):
                        tiles.append((si, t, Sl // 128, offs[si] + t * 128))
                NTT = len(tiles)

                def stage_L(i):
                    s_ = i % NX
                    g0 = tiles[i][3]
                    op("sp", lambda e: e.dma_start(out=xs[s_][:], in_=x_d[g0:g0 + 128, :]), writes=[b_xs[s_]], dma=b_xs[s_])

                def stage_Na(i):
                    s_ = i % NX; h_ = i % 3
                    op("act", lambda e: e.activation(out=junk[:], in_=xs[s_][:], func=AF.Square, accum_out=ss1[h_][:]),
                       reads=[b_xs[s_]], writes=[b_junk, b_ss1[h_]])
                    rsqrt_ops("p1", ss1[h_][:], ss1[h_][:], 1, 1.0 / D, [b_ss1[h_]], [b_ss1[h_]])

                def stage_Nb(i):
                    s_ = i % NX; h_ = i % 3
                    op("act", lambda e: e.activation(out=hb[h_][:], in_=xs[s_][:], func=AF.Copy, scale=ss1[h_][:]),
                       reads=[b_xs[s_], b_ss1[h_]], writes=[b_hb[h_]])

                def stage_T(i):
                    h_ = i % 2
                    h3 = i % 3
                    for k in range(8):
                        op("pe", lambda e, k=k: e.transpose(out=pTp[:, k, :], in_=hb[h3][:, k * 128:(k + 1) * 128], identity=ident[:]),
                           reads=[b_hb[h3], b_ident], writes=[b_pTp])
                    op("dve", lambda e: e.tensor_copy(out=hT[h_][:], in_=pTp[:]), reads=[b_pTp], writes=[b_hT[h_]])

                def stage_M(i):
                    h_ = i % 2
                    for grp in ((0, 1), (2, 3)):
                        for k in range(8):
                            for n in grp:
                                op("pe", lambda e, k=k, n=n: e.matmul(zps[n][:], lhsT=hT[h_][:, k, :],
                                                                      rhs=w_in_bf[:, k, n * 512:(n + 1) * 512],
                                                                      start=(k == 0), stop=(k == 7)),
                                   reads=[b_hT[h_], b_win], writes=[b_z[n]])

                def stage_E(i):
                    si, t, nt, g0 = tiles[i]
                    h_ = i % 2
                    for j in range(2):
                        op("act", lambda e, j=j: e.activation(out=zsb[j][h_][:], in_=zps[j][:], func=AF.Copy),
                           reads=[b_z[j]], writes=[b_zsb[j][h_]])
                    vblk = g0 // 512
                    vsub = (g0 % 512) // 128
                    vs_ = vblk % 2
                    op("act", lambda e: e.activation(out=vst[vs_][:, :, vsub, 0:128], in_=zps[2][:].rearrange("p (h e) -> p h e", e=128), func=AF.Copy),
                       reads=[b_z[2]], writes=[b_vst[vs_]])
                    if vsub == 3 or t == nt - 1:
                        c0 = vblk * 4
                        op("sp", lambda e: e.dma_start(out=V_d[:, :, c0:c0 + vsub + 1, :].rearrange("h p c e -> p h c e"),
                                                         in_=vst[vs_][:, :, 0:vsub + 1, :]),
                           reads=[b_vst[vs_]], writes=[DB("V", vblk)], dma=b_vst[vs_])
                    ps_ = i % NPB
                    op("act", lambda e: e.activation(out=pb[ps_][:], in_=zps[3][:], func=AF.Copy), reads=[b_z[3]], writes=[b_pb[ps_]])

                def stage_E2(i):
                    si, t, nt, g0 = tiles[i]
                    h_ = i % 2
                    for j in range(2):
                        Ct, St = ((Cq, Sq), (Ck, Sk))[j]
                        z = zsb[j][h_][:].rearrange("p (a b) -> p a b", b=64)
                        zb = b_zsb[j][h_]
                        A, Bq, Q = ta[j][h_], tb[j][h_], sq[j][h_]
                        op("act", lambda e, z=z, Q=Q: e.activation(out=Q[:], in_=z, func=AF.Square), reads=[zb], writes=[b_sq[j][h_]])
                        cb = Ct[:, t, :].unsqueeze(1).to_broadcast([128, 8, 64])
                        op("dve", lambda e, z=z, A=A, cb=cb: e.tensor_tensor(out=A[:], in0=z, in1=cb, op=ALU.mult),
                           reads=[zb, b_tab], writes=[b_ta[j][h_]])
                        for hf in range(2):
                            lo, hi = hf * 32, hf * 32 + 32
                            olo, ohi = (1 - hf) * 32, (1 - hf) * 32 + 32
                            sb = St[:, t, lo:hi].unsqueeze(1).to_broadcast([128, 8, 32])
                            op("dve", lambda e, z=z, Bq=Bq, sb=sb, lo=lo, hi=hi, olo=olo, ohi=ohi:
                               e.tensor_tensor(out=Bq[:, :, lo:hi], in0=z[:, :, olo:ohi], in1=sb, op=ALU.mult),
                               reads=[zb, b_tab], writes=[b_tb[j][h_]])
                    R = rs[h_]
                    for j in range(2):
                        Q = sq[j][h_]
                        op("dve", lambda e, Q=Q, j=j: e.tensor_reduce(out=R[:, j, :], in_=Q[:], axis=AX.X, op=ALU.add),
                           reads=[b_sq[j][h_]], writes=[b_rs[h_]])
                    Rf = R[:].rearrange("p a b -> p (a b)")
                    op("dve", lambda e: e.tensor_scalar(out=Rf, in0=Rf, scalar1=1.0 / 64, scalar2=EPS, op0=ALU.mult, op1=ALU.add),
                       reads=[b_rs[h_]], writes=[b_rs[h_]])
                    op("act", lambda e: e.activation(out=Rf, in_=Rf, func=AF.Ln), reads=[b_rs[h_]], writes=[b_rs[h_]])
                    op("act", lambda e: e.activation(out=Rf, in_=Rf, func=AF.Exp, scale=-0.5), reads=[b_rs[h_]], writes=[b_rs[h_]])
                    for j in range(2):
                        A, Bq = ta[j][h_], tb[j][h_]
                        op("dve", lambda e, A=A, Bq=Bq: e.tensor_tensor(out=A[:], in0=A[:], in1=Bq[:], op=ALU.add),
                           reads=[b_ta[j][h_], b_tb[j][h_]], writes=[b_ta[j][h_]])
                    for j in range(2):
                        A = ta[j][h_]
                        rb = R[:, j, :].unsqueeze(2).to_broadcast([128, 8, 64])
                        dst = qkb[j][h_][:].rearrange("p (a b) -> p a b", b=64)
                        op("dve", lambda e, A=A, rb=rb, dst=dst: e.tensor_tensor(out=dst, in0=A[:], in1=rb, op=ALU.mult),
                           reads=[b_ta[j][h_], b_rs[h_]], writes=[b_qkb[j][h_]])

                def stage_Q(i):
                    si, t, nt, g0 = tiles[i]
                    h_ = i % 2
                    blk = g0 // 512
                    sub = (g0 % 512) // 128
                    st_ = blk % 2
                    for j in range(2):
                        for hh in range(4):
                            op("pe", lambda e, j=j, hh=hh: e.transpose(out=qkTp[:, j, hh, :], in_=qkb[j][h_][:, hh * 128:(hh + 1) * 128],
                                                                     identity=ident[:]),
                               reads=[b_qkb[j][h_], b_ident], writes=[b_qkTp])
                    for j in range(2):
                        eng = "act"
                        if eng == "dve":
                            op("dve", lambda e, j=j: e.tensor_copy(out=qst[j][st_][:, :, sub * 128:(sub + 1) * 128], in_=qkTp[:, j, :, :]),
                               reads=[b_qkTp], writes=[b_qst[j][st_]])
                        else:
                            op("act", lambda e, j=j: e.activation(out=qst[j][st_][:, :, sub * 128:(sub + 1) * 128], in_=qkTp[:, j, :, :], func=AF.Copy),
                               reads=[b_qkTp], writes=[b_qst[j][st_]])
                    last_in_blk = (sub == 3) or (t == nt - 1)
                    if last_in_blk:
                        n = (sub + 1) * 128
                        b0 = blk * 512
                        for j, dst in enumerate((QT_d, KT_d)):
                            op("sp", lambda e, j=j, dst=dst: e.dma_start(out=dst[:, :, b0:b0 + n].rearrange("h p t -> p h t"),
                                                                          in_=qst[j][st_][:, :, 0:n]),
                               reads=[b_qst[j][st_]], writes=[DB("QT" if j == 0 else "KT", blk)], dma=b_qst[j][st_])

                def stage_P(i):
                    si, t, nt, g0 = tiles[i]
                    blk = g0 // 512
                    sub = (g0 % 512) // 128
                    st_ = blk % 2
                    srcs = []
                    if t > 0:
                        srcs.append((i - 1, 0))
                    srcs.append((i, 3 if t == 0 else (4 if t == nt - 1 else 1)))
                    if t < nt - 1:
                        srcs.append((i + 1, 2))
                    for g in range(4):
                        for n_, (ii, kind) in enumerate(srcs):
                            op("pe", lambda e, g=g, ii=ii, kind=kind, n_=n_: e.matmul(
                                plp[:, g, :], lhsT=pb[ii % NPB][:, g * 128:(g + 1) * 128], rhs=band[:, g * 5 + kind, :],
                                start=(n_ == 0), stop=(n_ == len(srcs) - 1)),
                               reads=[b_pb[ii % NPB], b_band], writes=[b_plp])
                    op("dve", lambda e: e.tensor_copy(out=pgT[:], in_=plp[:]), reads=[b_plp], writes=[b_pgT])
                    for g in range(4):
                        op("pe", lambda e, g=g: e.matmul(opp[:, g, :], lhsT=wpl[:, g, :], rhs=pgT[:, g, :], start=True, stop=True),
                           reads=[b_pgT, b_wpl], writes=[b_opp])
                    for g in range(4):
                        op("dve", lambda e, g=g: e.tensor_scalar(out=ost[st_][:, g, sub * 128:(sub + 1) * 128], in0=opp[:, g, :],
                                                                 scalar1=psc[:, g:g + 1], scalar2=None, op0=ALU.mult),
                           reads=[b_opp, b_wpl], writes=[b_ost[st_]])
                    if sub == 3 or t == nt - 1:
                        n = (sub + 1) * 128
                        b0 = blk * 512
                        op("sp", lambda e: e.dma_start(out=MT_d[4:8, :, b0:b0 + n].rearrange("h p t -> p h t"), in_=ost[st_][:, :, 0:n]),
                           reads=[b_ost[st_]], writes=[DB("MTP", blk)], dma=b_ost[st_])

                for i_ in range(min(3, NTT)):
                    stage_L(i_)
                stage_Na(0); stage_Nb(0)
                if NTT > 1:
                    stage_Na(1); stage_Nb(1)
                stage_T(0)
                for i in range(NTT + 2):
                    if i + 3 < NTT:
                        stage_L(i + 3)
                    if i + 2 < NTT:
                        stage_Na(i + 2)
                    if i < NTT:
                        stage_M(i)
                    if i + 1 < NTT:
                        stage_T(i + 1)
                    if 0 <= i - 2 < NTT:
                        stage_Q(i - 2)
                        stage_P(i - 2)
                    if i < NTT:
                        stage_E(i)
                    if i + 2 < NTT:
                        stage_Nb(i + 2)
                    if i < NTT:
                        stage_E2(i)
                S.fence()

        with ExitStack() as p2:
            if _STOP < 2:
                S.muted = True
            NCMAX = SMAX // 128
            KTs = [T(p2, "KTs%d" % i, [128, SMAX], BF16) for i in range(2)]; b_KT = [Buf(), Buf()]
            Vx = [T(p2, "Vx%d" % i, [128, NCMAX, 129], BF16) for i in range(2)]; b_V = [Buf(), Buf()]
            NQ = 3
            Qs = [T(p2, "Qs%d" % i, [128, 512], BF16) for i in range(NQ)]; b_Q = [Buf() for _ in range(NQ)]
            NE = 3
            Es = [T(p2, "Es%d" % i, [128, 1024], BF16) for i in range(NE)]; b_E = [Buf() for _ in range(NE)]
            Oc = T(p2, "Oc", [128, 8, 129], F32); b_Oc = Buf()
            rr = [T(p2, "rr%d" % i, [128, 2], F32) for i in range(2)]; b_rr = [Buf(), Buf()]
            t0s = [T(p2, "t0s%d" % i, [128, 128], F32) for i in range(2)]; b_t0 = [Buf(), Buf()]
            osq = [T(p2, "osq%d" % i, [128, 128], F32) for i in range(2)]; b_osq = [Buf(), Buf()]
            oss = [T(p2, "oss%d" % i, [128, 1], F32) for i in range(2)]; b_oss = [Buf(), Buf()]
            onb = [T(p2, "onb%d" % i, [128, 128], BF16) for i in range(4)]; b_onb = [Buf() for _ in range(4)]
            oat = [T(p2, "oat%d" % i, [128, 512], BF16) for i in range(2)]; b_oat = [Buf(), Buf()]
            with ExitStack() as pp:
                Sp = [P(pp, "Sp%d" % i, [128, 1024]) for i in range(2)]; b_Sp = [Buf(psum=True), Buf(psum=True)]
                Op_ = [P(pp, "Op%d" % i, [128, 512]) for i in range(3)]; b_Op = [Buf(psum=True), Buf(psum=True), Buf(psum=True)]
                oTp = P(pp, "oTp", [128, 8, 128], BF16); b_oTp = Buf(psum=True)

                def acc(a):
                    return Op_[a // 3][:, (a % 3) * 129:(a % 3 + 1) * 129], b_Op[a // 3]

                heads = [(si, h) for si in range(len(seqs)) for h in range(4)]

                def load_head(n):
                    si, h = heads[n]
                    Sl = seqs[si]; o0 = offs[si]; s_ = n % 2
                    rd = [DB("KT", (o0 + b) // 512) for b in range(0, Sl, 512)]
                    op("sp", lambda e: e.dma_start(out=KTs[s_][:, 0:Sl], in_=KT_d[h, :, o0:o0 + Sl]),
                       reads=rd, writes=[b_KT[s_]], dma=b_KT[s_])
                    rd = [DB("V", (o0 + b) // 512) for b in range(0, Sl, 512)]
                    op("sp", lambda e: e.dma_start(out=Vx[s_][:, 0:Sl // 128, :], in_=V_d[h, :, o0 // 128:(o0 + Sl) // 128, :]),
                       reads=rd, writes=[b_V[s_]], dma=b_V[s_])

                qtiles = []
                for n, (si, h) in enumerate(heads):
                    Sl = seqs[si]
                    QTL = min(512, Sl)
                    for q0 in range(0, Sl, QTL):
                        qtiles.append((n, si, h, q0, QTL))

                def load_q(qi):
                    n, si, h, q0, QTL = qtiles[qi]
                    g0 = offs[si] + q0
                    s_ = qi % NQ
                    op("sp", lambda e: e.dma_start(out=Qs[s_][:, 0:QTL], in_=QT_d[h, :, g0:g0 + QTL]),
                       reads=[DB("QT", g0 // 512)], writes=[b_Q[s_]], dma=b_Q[s_])

                ectr = [0]

                def S_step(qi, kc, b):
                    n, si, h, q0, QTL = qtiles[qi]
                    s_ = n % 2; qs_ = qi % NQ
                    op("pe", lambda e: e.matmul(Sp[b][:, 0:QTL], lhsT=KTs[s_][0:64, kc * 128:(kc + 1) * 128],
                                                rhs=Qs[qs_][0:64, 0:QTL], start=True, stop=True, tile_position=(0, 0)),
                       reads=[b_KT[s_], b_Q[qs_]], writes=[b_Sp[b]])
                    op("pe", lambda e: e.matmul(Sp[b][:, 512:512 + QTL], lhsT=KTs[s_][64:128, kc * 128:(kc + 1) * 128],
                                                rhs=Qs[qs_][64:128, 0:QTL], start=True, stop=True, tile_position=(64, 0)),
                       reads=[b_KT[s_], b_Q[qs_]], writes=[b_Sp[b]])

                def E_step(qi, kc, b):
                    n, si, h, q0, QTL = qtiles[qi]
                    es_ = ectr[0] % NE
                    ectr[0] += 1
                    if QTL == 512:
                        op("act", lambda e: e.activation(out=Es[es_][:], in_=Sp[b][:], func=AF.Exp, bias=negB[:]),
                           reads=[b_Sp[b], b_negB], writes=[b_E[es_]])
                    else:
                        for m in range(2):
                            op("act", lambda e, m=m: e.activation(out=Es[es_][:, m * 512:m * 512 + QTL],
                                                                   in_=Sp[b][:, m * 512:m * 512 + QTL], func=AF.Exp, bias=negB[:]),
                               reads=[b_Sp[b], b_negB], writes=[b_E[es_]])
                    return es_

                def PV_step(qi, kc, es_):
                    n, si, h, q0, QTL = qtiles[qi]
                    s_ = n % 2
                    NC_ = seqs[si] // 128
                    for qq in range(QTL // 128):
                        for m in range(2):
                            ap_, bb = acc(qq * 2 + m)
                            first = (kc == 0 and (qq * 2 + m) % 3 == 0)
                            op("pe", lambda e, qq=qq, m=m, ap_=ap_, first=first: e.matmul(
                                ap_, lhsT=Es[es_][:, m * 512 + qq * 128:m * 512 + (qq + 1) * 128],
                                rhs=Vx[s_][:, kc, :], start=first, stop=(kc == NC_ - 1), skip_group_check=True),
                               reads=[b_E[es_], b_V[s_]], writes=[bb])

                def finalize_a(qi):
                    n, si, h, q0, QTL = qtiles[qi]
                    NQQ = QTL // 128
                    for bnk in range(3):
                        na = min(3, NQQ * 2 - bnk * 3)
                        if na <= 0:
                            continue
                        op("dve", lambda e, bnk=bnk, na=na: e.tensor_copy(out=Oc[:, bnk * 3:bnk * 3 + na, :], in_=Op_[bnk][:, 0:na * 129].rearrange("p (a b) -> p a b", b=129)),
                           reads=[b_Op[bnk]], writes=[b_Oc])

                def finalize_b(qi):
                    n, si, h, q0, QTL = qtiles[qi]
                    NQQ = QTL // 128
                    g0 = offs[si] + q0
                    o_ = qi % 2
                    for qq in range(NQQ):
                        p_ = qq % 2
                        op("dve", lambda e, qq=qq, p_=p_: e.reciprocal(out=rr[p_][:], in_=Oc[:, 2 * qq:2 * qq + 2, 128]),
                           reads=[b_Oc], writes=[b_rr[p_]])
                        op("dve", lambda e, p_=p_: e.tensor_scalar(out=rr[p_][:, 1:2], in0=rr[p_][:, 1:2], scalar1=neglam[:, 0:1],
                                                                  scalar2=None, op0=ALU.mult),
                           reads=[b_rr[p_], b_lam], writes=[b_rr[p_]])
                        op("dve", lambda e, qq=qq, p_=p_: e.tensor_scalar(out=t0s[p_][:], in0=Oc[:, 2 * qq, 0:128], scalar1=rr[p_][:, 0:1],
                                                                         scalar2=None, op0=ALU.mult),
                           reads=[b_Oc, b_rr[p_]], writes=[b_t0[p_]])
                        op("dve", lambda e, qq=qq, p_=p_: e.scalar_tensor_tensor(out=t0s[p_][:], in0=Oc[:, 2 * qq + 1, 0:128],
                                                                                scalar=rr[p_][:, 1:2], in1=t0s[p_][:],
                                                                                op0=ALU.mult, op1=ALU.add),
                           reads=[b_Oc, b_rr[p_], b_t0[p_]], writes=[b_t0[p_]])
                        op("pool", lambda e, p_=p_: e.tensor_tensor(out=osq[p_][:], in0=t0s[p_][:], in1=t0s[p_][:], op=ALU.mult),
                           reads=[b_t0[p_]], writes=[b_osq[p_]])
                        op("dve", lambda e, p_=p_: e.tensor_reduce(out=oss[p_][:], in_=osq[p_][:], axis=AX.X, op=ALU.add),
                           reads=[b_osq[p_]], writes=[b_oss[p_]])
                        rsqrt_ops("p2", oss[p_][:], oss[p_][:], 1, 1.0 / 128, [b_oss[p_]], [b_oss[p_]])
                        op("dve", lambda e, p_=p_, qq=qq: e.tensor_scalar(out=onb[qq][:], in0=t0s[p_][:], scalar1=oss[p_][:, 0:1],
                                                                         scalar2=None, op0=ALU.mult),
                           reads=[b_t0[p_], b_oss[p_]], writes=[b_onb[qq]])

                def finalize_c(qi):
                    n, si, h, q0, QTL = qtiles[qi]
                    NQQ = QTL // 128
                    g0 = offs[si] + q0
                    o_ = qi % 2
                    for qq in range(NQQ):
                        op("pe", lambda e, qq=qq: e.transpose(out=oTp[:, qq, :], in_=onb[qq][:], identity=ident[:]),
                           reads=[b_onb[qq], b_ident], writes=[b_oTp])
                    op("dve", lambda e: e.tensor_copy(out=oat[o_][:, 0:QTL], in_=oTp[:, 0:NQQ, :].rearrange("p a b -> p (a b)")),
                       reads=[b_oTp], writes=[b_oat[o_]])
                    op("sp", lambda e: e.dma_start(out=MT_d[h, :, g0:g0 + QTL], in_=oat[o_][:, 0:QTL]),
                       reads=[b_oat[o_]], writes=[DB("MTA%d" % h, g0 // 512)], dma=b_oat[o_])

                G = []
                for qi, (n, si, h, q0, QTL) in enumerate(qtiles):
                    for kc in range(seqs[si] // 128):
                        G.append((qi, kc))
                load_head(0)
                load_q(0)
                if len(qtiles) > 1:
                    load_q(1)
                if len(heads) > 1:
                    load_head(1)
                cur_head = 0
                for g_ in range(min(2, len(G))):
                    S_step(G[g_][0], G[g_][1], g_ % 2)
                for g_, (qi, kc) in enumerate(G):
                    n, si, h, q0, QTL = qtiles[qi]
                    NC_ = seqs[si] // 128
                    if kc == 0:
                        if n != cur_head:
                            cur_head = n
                            if n + 1 < len(heads):
                                load_head(n + 1)
                        if qi + 2 < len(qtiles):
                            load_q(qi + 2)
                        if qi > 0:
                            finalize_a(qi - 1)
                    es_ = E_step(qi, kc, g_ % 2)
                    if g_ + 2 < len(G):
                        S_step(G[g_ + 2][0], G[g_ + 2][1], g_ % 2)
                    PV_step(qi, kc, es_)
                    if qi > 0 and kc == min(2, NC_ - 1):
                        finalize_b(qi - 1)
                    if qi > 0 and kc == min(14, NC_ - 1):
                        finalize_c(qi - 1)
                finalize_a(len(qtiles) - 1)
                finalize_b(len(qtiles) - 1)
                finalize_c(len(qtiles) - 1)
                S.fence()

        with ExitStack() as p3:
            if _STOP < 3:
                S.muted = True
            w_out_bf = T(p3, "w_out_bf", [128, 8, D], BF16)
            w_up_bf = T(p3, "w_up_bf", [128, 8, 2 * DFF], BF16)
            w_dn_bf = T(p3, "w_dn_bf", [128, NFC, D], BF16)
            g2 = T(p3, "g2", [128, 8], F32); sg = T(p3, "sg", [128, 1], F32)
            cw = T(p3, "cw", [128, 3, 2 * NFC], F32); cbias = T(p3, "cbias", [128, 2 * NFC], F32)
            b_w3 = Buf("w3"); b_g2 = Buf("g2")
            op("sp", lambda e: e.dma_start(out=g2[:], in_=norm2_g.rearrange("(k p) -> p k", p=128), allow_slow_non_contiguous=True), writes=[b_g2], dma=b_g2)
            op("sp", lambda e: e.dma_start(out=sg[:], in_=subln_g.rearrange("(p o) -> p o", o=1)), writes=[b_g2], dma=b_g2)
            pq0 = ExitStack()
            cwT = T(pq0, "cwT", [2 * NFC, 4, 128], F32); b_cwT = Buf("cwT")
            op("sp", lambda e: e.dma_start(out=cwT[:, 0:3, :], in_=conv_w.rearrange("k (c p) -> c k p", p=128)), writes=[b_cwT], dma=b_cwT)
            op("sp", lambda e: e.dma_start(out=cwT[:, 3, :], in_=conv_b.rearrange("(c p) -> c p", p=128)), writes=[b_cwT], dma=b_cwT)
            with ExitStack() as pq:
                cwp = P(pq, "cwp", [128, 4, 128]); b_cwp = Buf("cwp", psum=True)
                for k in range(4):
                    op("pe", lambda e, k=k: e.transpose(out=cwp[:, k, 0:2 * NFC], in_=cwT[:, k, :], identity=identf[0:2 * NFC, 0:2 * NFC]),
                       reads=[b_cwT, b_ident], writes=[b_cwp])
                op("dve", lambda e: e.tensor_copy(out=cw[:], in_=cwp[:, 0:3, 0:2 * NFC]), reads=[b_cwp], writes=[b_g2])
                op("dve", lambda e: e.tensor_copy(out=cbias[:], in_=cwp[:, 3, 0:2 * NFC]), reads=[b_cwp], writes=[b_g2])
            op("dve", lambda e: e.tensor_scalar(out=sg[:], in0=sg[:], scalar1=1.0 - LAMBDA_INIT, scalar2=None, op0=ALU.mult),
               reads=[b_g2], writes=[b_g2])
            with ExitStack() as pw:
                WS = 1408
                NWS = 8
                wst3 = [T(pw, "wst3_%d" % i, [128, WS], F32) for i in range(NWS)]; b_wst3 = [Buf() for _ in range(NWS)]
                cnt = [0]

                def prep(dst_ap_fn, src_ap, ncols, scal):
                    for c0 in range(0, ncols, WS):
                        n = min(WS, ncols - c0)
                        s_ = cnt[0] % NWS
                        eng = ("dve", "act")[cnt[0] % 2]
                        cnt[0] += 1
                        op("sp", lambda e, c0=c0, n=n, s_=s_: e.dma_start(out=wst3[s_][:, 0:n], in_=src_ap[:, c0:c0 + n]),
                           writes=[b_wst3[s_]], dma=b_wst3[s_])
                        if eng == "act":
                            op("act", lambda e, c0=c0, n=n, s_=s_: e.activation(out=dst_ap_fn(c0, n), in_=wst3[s_][:, 0:n], func=AF.Copy,
                                                                              scale=(scal if scal is not None else 1.0)),
                               reads=[b_wst3[s_], b_g2], writes=[b_w3])
                        elif scal is None:
                            op(eng, lambda e, c0=c0, n=n, s_=s_: e.tensor_copy(out=dst_ap_fn(c0, n), in_=wst3[s_][:, 0:n]),
                               reads=[b_wst3[s_]], writes=[b_w3])
                        else:
                            op(eng, lambda e, c0=c0, n=n, s_=s_: e.tensor_scalar(out=dst_ap_fn(c0, n), in0=wst3[s_][:, 0:n],
                                                                               scalar1=scal, scalar2=None, op0=ALU.mult),
                               reads=[b_wst3[s_], b_g2], writes=[b_w3])
                for k in range(8):
                    prep(lambda c0, n, k=k: w_out_bf[:, k, c0:c0 + n], w_out[k * 128:(k + 1) * 128, :], D,
                         sg[:, 0:1] if k < 4 else None)
                for k in range(8):
                    prep(lambda c0, n, k=k: w_up_bf[:, k, c0:c0 + n], w_up[k * 128:(k + 1) * 128, :], 2 * DFF, g2[:, k:k + 1])
                for k in range(NFC):
                    prep(lambda c0, n, k=k: w_dn_bf[:, k, c0:c0 + n], w_down[k * 128:(k + 1) * 128, :], D, None)
                S.fence()
            pq0.close()
            b_w3.const = True; b_g2.const = True

            TL = 256
            NG = 5
            x1 = [T(p3, "x1_%d" % i, [128, D], F32) for i in range(NG)]; b_x1 = [Buf() for _ in range(NG)]
            mixT = T(p3, "mixT", [128, 8, TL], BF16); b_mix = Buf()
            h2T = [T(p3, "h2T%d" % i, [128, 8, TL + 2], BF16) for i in range(2)]; b_h2T = [Buf(), Buf()]
            actT = T(p3, "actT", [128, NFC, TL], BF16); b_actc = [Buf() for _ in range(NFC)]
            ss3 = [T(p3, "ss3_%d" % i, [128, 1], F32) for i in range(2)]; b_ss3 = [Buf(), Buf()]
            hb3 = [T(p3, "hb3_%d" % i, [128, D], BF16) for i in range(2)]; b_hb3 = [Buf(), Buf()]
            cg = [T(p3, "cg%d" % i, [128, TL], F32) for i in range(2)]; b_cg = [Buf(), Buf()]
            cv = [T(p3, "cv%d" % i, [128, TL], F32) for i in range(3)]; b_cv = [Buf(), Buf(), Buf()]
            sgt = [T(p3, "sgt%d" % i, [128, TL], F32) for i in range(2)]; b_sgt = [Buf(), Buf()]
            with ExitStack() as pp:
                NUP = 5
                NWP = 2
                Up = [P(pp, "Up%d" % i, [128, 512]) for i in range(NUP)]; b_Up = [Buf(psum=True) for _ in range(NUP)]
                Wp = [P(pp, "Wp%d" % i, [128, 512]) for i in range(NWP)]; b_Wp = [Buf(psum=True) for _ in range(NWP)]
                uctr = [0]
                tTp = P(pp, "tTp", [128, 8, 128], BF16); b_tTp = Buf(psum=True)
                wctr = [0]
                t3 = []
                for si, Sl in enumerate(seqs):
                    for t in range(Sl // TL):
                        t3.append((si, t, Sl // TL, offs[si] + t * TL))
                N3 = len(t3)
                gctr = {}

                def gslot(i, g):
                    return (2 * i + g) % NG

                def load_x(i, g):
                    if i >= N3:
                        return
                    si, t, nt, g0 = t3[i]
                    s_ = gslot(i, g)
                    op("sp", lambda e: e.dma_start(out=x1[s_][:], in_=x_d[g0 + g * 128:g0 + (g + 1) * 128, :]),
                       writes=[b_x1[s_]], dma=b_x1[s_])

                def load_mix(i):
                    if i >= N3:
                        return
                    si, t, nt, g0 = t3[i]
                    rd = [DB("MTP", g0 // 512)] + [DB("MTA%d" % h, g0 // 512) for h in range(4)]
                    op("sp", lambda e: e.dma_start(out=mixT[:], in_=MT_d[:, :, g0:g0 + TL].rearrange("c p t -> p c t")),
                       reads=rd, writes=[b_mix], dma=b_mix)

                def A_T(i, g):
                    hs = i % 2
                    p_ = g
                    for k in range(8):
                        op("pe", lambda e, k=k, p_=p_: e.transpose(out=tTp[:, k, :], in_=hb3[p_][:, k * 128:(k + 1) * 128], identity=ident[:]),
                           reads=[b_hb3[p_], b_ident], writes=[b_tTp])
                    op("dve", lambda e, g=g: e.tensor_copy(out=h2T[hs][:, :, 1 + g * 128:1 + (g + 1) * 128], in_=tTp[:]),
                       reads=[b_tTp], writes=[b_h2T[hs]])

                def A1(i, g):
                    if i >= N3:
                        return
                    s_ = gslot(i, g)
                    p_ = g
                    for half in range(2):
                        w_ = wctr[0] % NWP
                        wctr[0] += 1
                        for k in range(8):
                            op("pe", lambda e, k=k, half=half, w_=w_: e.matmul(
                                Wp[w_][:], lhsT=mixT[:, k, g * 128:(g + 1) * 128], rhs=w_out_bf[:, k, half * 512:(half + 1) * 512],
                                start=(k == 0), stop=(k == 7)), reads=[b_mix, b_w3], writes=[b_Wp[w_]])
                        op("dve", lambda e, half=half, w_=w_: e.tensor_tensor(
                            out=x1[s_][:, half * 512:(half + 1) * 512], in0=Wp[w_][:], in1=x1[s_][:, half * 512:(half + 1) * 512], op=ALU.add),
                           reads=[b_Wp[w_], b_x1[s_]], writes=[b_x1[s_]])
                    op("act", lambda e: e.activation(out=hb3[p_][:], in_=x1[s_][:], func=AF.Square, accum_out=ss3[p_][:]),
                       reads=[b_x1[s_]], writes=[b_hb3[p_], b_ss3[p_]])
                    rsqrt_ops("p3", ss3[p_][:], ss3[p_][:], 1, 1.0 / D, [b_ss3[p_]], [b_ss3[p_]])
                    op("act", lambda e: e.activation(out=hb3[p_][:], in_=x1[s_][:], func=AF.Copy, scale=ss3[p_][:]),
                       reads=[b_x1[s_], b_ss3[p_]], writes=[b_hb3[p_]])

                def A_halo(i):
                    si, t, nt, g0 = t3[i]
                    hs = i % 2
                    if t == 0:
                        op("dve", lambda e: e.memset(h2T[hs][:, :, 0:1], 0.0), writes=[b_h2T[hs]])
                    else:
                        op("dve", lambda e: e.tensor_copy(out=h2T[hs][:, :, 0:1], in_=h2T[1 - hs][:, :, TL:TL + 1]),
                           reads=[b_h2T[1 - hs]], writes=[b_h2T[hs]])
                        op("dve", lambda e: e.tensor_copy(out=h2T[1 - hs][:, :, TL + 1:TL + 2], in_=h2T[hs][:, :, 1:2]),
                           reads=[b_h2T[hs]], writes=[b_h2T[1 - hs]])
                    if t == nt - 1:
                        op("dve", lambda e: e.memset(h2T[hs][:, :, TL + 1:TL + 2], 0.0), writes=[b_h2T[hs]])

                def stage_Bup(i, tail=None):
                    hs = i % 2
                    NW = TL + 2
                    pend = []
                    for j in range(NFC + 1):
                        if j == NFC:
                            pend.pop(0)()
                            break
                        if j == 4 and tail is not None:
                            tail()
                        ubs = (uctr[0] % NUP, (uctr[0] + 1) % NUP)
                        uctr[0] += 2
                        for which, c in ((0, j), (1, j + NFC)):
                            ub = ubs[which]
                            for k in range(8):
                                op("pe", lambda e, k=k, c=c, ub=ub: e.matmul(
                                    Up[ub][:, 0:NW], lhsT=w_up_bf[:, k, c * 128:(c + 1) * 128], rhs=h2T[hs][:, k, :],
                                    start=(k == 0), stop=(k == 7)), reads=[b_h2T[hs], b_w3], writes=[b_Up[ub]])
                        p_ = j % 2
                        pv_ = j % 3
                        gv = ((0, j, cg, b_cg, p_), (1, j + NFC, cv, b_cv, pv_))
                        for which, c, dstl, dbl, sl_ in gv:
                            ub = ubs[which]
                            dst = dstl[sl_]
                            op("act", lambda e, c=c, ub=ub, dst=dst: e.activation(
                                out=dst[:], in_=Up[ub][:, 1:TL + 1], func=AF.Identity, scale=cw[:, 1, c:c + 1], bias=cbias[:, c:c + 1]),
                               reads=[b_Up[ub], b_g2], writes=[dbl[sl_]])
                        if pend:
                            pend.pop(0)()
                        for (lo_, kk) in ((0, 0), (2, 2)):
                            for which, c, dstl, dbl, sl_ in gv:
                                ub = ubs[which]
                                dst = dstl[sl_]
                                op("dve", lambda e, c=c, ub=ub, dst=dst, lo_=lo_, kk=kk: e.scalar_tensor_tensor(
                                    out=dst[:], in0=Up[ub][:, lo_:lo_ + TL], scalar=cw[:, kk, c:c + 1], in1=dst[:], op0=ALU.mult, op1=ALU.add),
                                   reads=[b_Up[ub], b_g2, dbl[sl_]], writes=[dbl[sl_]])
                        def gate_mul(p_=p_, pv_=pv_, j=j):
                            op("act", lambda e: e.activation(out=sgt[p_][:], in_=cg[p_][:], func=AF.Silu),
                               reads=[b_cg[p_]], writes=[b_sgt[p_]])
                            op("pool", lambda e: e.tensor_tensor(out=actT[:, j, :], in0=sgt[p_][:], in1=cv[pv_][:], op=ALU.mult),
                               reads=[b_sgt[p_], b_cv[pv_]], writes=[b_actc[j]])
                        pend.append(gate_mul)

                def Bdn(i, g):
                    si, t, nt, g0 = t3[i]
                    s_ = gslot(i, g)
                    for half in range(2):
                        w_ = wctr[0] % NWP
                        wctr[0] += 1
                        for c in range(NFC):
                            op("pe", lambda e, c=c, half=half, w_=w_: e.matmul(
                                Wp[w_][:], lhsT=actT[:, c, g * 128:(g + 1) * 128], rhs=w_dn_bf[:, c, half * 512:(half + 1) * 512],
                                start=(c == 0), stop=(c == NFC - 1)), reads=[b_actc[c], b_w3], writes=[b_Wp[w_]])
                        op("dve", lambda e, half=half, w_=w_: e.tensor_tensor(
                            out=x1[s_][:, half * 512:(half + 1) * 512], in0=Wp[w_][:], in1=x1[s_][:, half * 512:(half + 1) * 512], op=ALU.add),
                           reads=[b_Wp[w_], b_x1[s_]], writes=[b_x1[s_]])
                    op("sp", lambda e: e.dma_start(out=y_d[g0 + g * 128:g0 + (g + 1) * 128, :], in_=x1[s_][:]),
                       reads=[b_x1[s_]], dma=b_x1[s_])

                load_x(0, 0); load_x(0, 1); load_x(1, 0); load_x(1, 1)
                load_mix(0)
                A1(0, 0); A1(0, 1)
                load_mix(1)
                A_T(0, 0); A_halo(0); A_T(0, 1)
                A1(1, 0); A1(1, 1)
                load_mix(2)
                if N3 > 1:
                    A_T(1, 0); A_halo(1)
                for i in range(N3):
                    load_x(i + 2, 0)
                    if i + 1 < N3:
                        stage_Bup(i, tail=lambda i=i: A_T(i + 1, 1))
                    else:
                        stage_Bup(i)
                    Bdn(i, 0)
                    load_x(i + 2, 1)
                    A1(i + 2, 0)
                    Bdn(i, 1)
                    if i + 2 < N3:
                        A_T(i + 2, 0)
                        A_halo(i + 2)
                    A1(i + 2, 1)
                    load_mix(i + 3)
                out_bufs.extend(b_x1)
        S.emit(nc, final_waits=out_bufs)
    return nc


_CACHE = {}
_STOP = 9
_LIMIT = None


def _get_nc(seqs):
    key = tuple(seqs)
    if key not in _CACHE:
        _CACHE[key] = build(list(seqs))
    return _CACHE[key]


def _consts():
    invf = (np.float32(10000.0) ** (-np.arange(0, 64, 2, dtype=np.float32) / np.float32(64))).astype(np.float32)
    return {"bands": bands_const(), "invf": invf}


def run_cores(x_list, weights, seqs):
    nc = _get_nc(seqs)
    cst = _consts()
    in_maps = []
    for xc in x_list:
        m = {"x": np.ascontiguousarray(xc, dtype=np.float32)}
        m.update(weights)
        m.update(cst)
        in_maps.append(m)
    res = run_bass_kernel_spmd(nc, in_maps, core_ids=list(range(len(x_list))))
    return [r["y"] for r in res.results]


def kernel(x_prompt, x_sample, norm1_g, w_in, q_norm_g, k_norm_g, lambda_q1, lambda_k1, lambda_q2,
           lambda_k2, subln_g, w_pool, pool_scale, w_out, norm2_g, w_up, conv_w, conv_b, w_down):
    f = lambda a: np.ascontiguousarray(np.asarray(a, dtype=np.float32))
    weights = {
        "norm1_g": f(norm1_g)[0], "w_in": f(w_in)[0], "q_norm_g": f(q_norm_g)[0], "k_norm_g": f(k_norm_g)[0],
        "lambda_q1": f(lambda_q1)[0], "lambda_k1": f(lambda_k1)[0], "lambda_q2": f(lambda_q2)[0],
        "lambda_k2": f(lambda_k2)[0], "subln_g": f(subln_g)[0], "w_pool": f(w_pool)[0],
        "pool_scale": f(pool_scale)[0], "w_out": f(w_out)[0], "norm2_g": f(norm2_g)[0], "w_up": f(w_up)[0],
        "conv_w": f(conv_w)[0], "conv_b": f(conv_b)[0], "w_down": f(w_down)[0],
    }
    xp = np.asarray(x_prompt, dtype=np.float32)
    xs = np.asarray(x_sample, dtype=np.float32)
    NCORE = 8
    SP, SS = xp.shape[1], xs.shape[1]
    seqs = [SP, SP, SS]
    x_list = []
    for c in range(NCORE):
        x_list.append(np.concatenate([xp[2 * c], xp[2 * c + 1], xs[c]], axis=0))
    ys = run_cores(x_list, weights, seqs)
    yp = np.empty_like(xp)
    ysm = np.empty_like(xs)
    for c in range(NCORE):
        yp[2 * c] = ys[c][0:SP]
        yp[2 * c + 1] = ys[c][SP:2 * SP]
        ysm[c] = ys[c][2 * SP:]
    return (yp, ysm)
```

```python
from contextlib import ExitStack
import math
import numpy as np
import concourse.bass as bass
import concourse.mybir as mybir
from concourse.bass_utils import run_bass_kernel_spmd

F32 = mybir.dt.float32
BF16 = mybir.dt.bfloat16
I32 = mybir.dt.int32
ALU = mybir.AluOpType
AF = mybir.ActivationFunctionType
AX = mybir.AxisListType

D = 1024
DFF = 2816
NFC = DFF // 128
EPS = 1e-6
LAMBDA_INIT = 0.8 - 0.6 * math.exp(-0.3 * 0)
POOL_W = (2, 4, 8, 16)
ENGS = ("pe", "act", "dve", "pool", "sp")


class Buf:
    __slots__ = ("name", "w", "r", "const", "sem", "dcount", "psum")

    def __init__(self, name="", psum=False):
        self.name = name
        self.psum = psum
        self.w = None
        self.r = []
        self.const = False
        self.sem = None
        self.dcount = 0


class Op:
    __slots__ = ("eng", "idx", "fn", "deps", "signal", "val", "dslot", "dval", "waits")


class Sched:
    def __init__(self):
        self.ops = {e: [] for e in ENGS}
        self.dma_slots = []
        self.dma_ops = []
        self.fence_deps = None
        self.fence_pending = set()
        self.muted = False
        self.limit = None

    def fence(self):
        deps = [l[-1] for l in self.ops.values() if l] + self.dma_ops
        self.dma_ops = []
        self.fence_deps = deps
        self.fence_pending = set(ENGS)

    def op(self, eng, fn, reads=(), writes=(), dma=None):
        if self.muted:
            return None
        if self.limit is not None:
            self.limit -= 1
            if self.limit < 0:
                return None
        o = Op()
        o.eng = eng
        o.fn = fn
        o.signal = False
        o.val = None
        o.dslot = dma
        o.dval = None
        deps = set()
        if eng in self.fence_pending:
            deps.update(self.fence_deps)
            self.fence_pending.discard(eng)
        for b in reads:
            if b.w is not None:
                deps.add(b.w)
            if b.psum:
                deps.update(r for r in b.r if r.eng != eng)
        for b in writes:
            if b.w is not None:
                deps.add(b.w)
            deps.update(b.r)
        o.deps = deps
        for b in reads:
            if not b.const:
                b.r.append(o)
        for b in writes:
            b.w = o
            b.r = []
        o.idx = len(self.ops[eng])
        self.ops[eng].append(o)
        if dma is not None:
            if dma.sem is None:
                dma.sem = True
                self.dma_slots.append(dma)
            dma.dcount += 16
            o.dval = dma.dcount
            self.dma_ops.append(o)
        return o

    def emit(self, nc, final_waits=()):
        for e in ENGS:
            waited = {}
            for o in self.ops[e]:
                best = {}
                for d in o.deps:
                    if d.dslot is not None:
                        key = ("d", id(d.dslot))
                        v = d.dval
                    else:
                        if d.eng == "pe" and e == "pe":
                            continue
                        key = d.eng
                        v = d.idx
                    if key not in best or best[key][0] < v:
                        best[key] = (v, d)
                ws = []
                for key, (v, d) in best.items():
                    if waited.get(key, -1) >= v:
                        continue
                    waited[key] = v
                    if d.dslot is None:
                        d.signal = True
                    ws.append(d)
                o.waits = ws
                o.deps = None
        with ExitStack() as st:
            sems = {e: st.enter_context(nc.semaphore("s_" + e)) for e in ENGS}
            for i, sl in enumerate(self.dma_slots):
                sl.sem = st.enter_context(nc.semaphore("d%d" % i))
            for e in ENGS:
                c = 0
                for o in self.ops[e]:
                    if o.dslot is None and o.signal:
                        c += 1
                        o.val = c
                assert c < 65000, (e, c)
            block = st.enter_context(nc.Block())
            handles = {"pe": block.tensor, "act": block.scalar, "dve": block.vector,
                       "pool": block.gpsimd, "sp": block.sync}
            for e in ENGS:
                ops = self.ops[e]

                def body(eng, ops=ops, e=e):
                    for o in ops:
                        for d in o.waits:
                            if d.dslot is not None:
                                eng.wait_ge(d.dslot.sem, d.dval)
                            else:
                                eng.wait_ge(sems[d.eng], d.val)
                        ins = o.fn(eng)
                        if o.dslot is not None:
                            ins.then_inc(o.dslot.sem, 16)
                        elif o.signal:
                            ins.then_inc(sems[e], 1)
                    if e == "sp":
                        for b in self.dma_slots:
                            eng.wait_ge(b.sem, b.dcount)

                handles[e](body)


def bands_const():
    S = 384
    out = np.zeros((4, 5, 128, 128), np.float32)
    idx = np.arange(S)
    for g, w in enumerate(POOL_W):
        lo = np.maximum(idx - w // 2, 0)
        hi = np.minimum(idx + w // 2 - 1, S - 1)
        cnt = (hi - lo + 1).astype(np.float64)
        M = np.zeros((S, S))
        for i in range(S):
            M[lo[i]:hi[i] + 1, i] = 1.0 / cnt[i]
            M[i, i] -= 1.0
        out[g, 0] = M[0:128, 128:256]
        out[g, 1] = M[128:256, 128:256]
        out[g, 2] = M[256:384, 128:256]
        out[g, 3] = M[0:128, 0:128]
        out[g, 4] = M[256:384, 256:384]
    return out.reshape(20, 128, 128)


def build(seqs, debug=False):
    nc = bass.Bass("TRN2", target_bir_lowering=False)
    TOK = sum(seqs)
    offs = [sum(seqs[:i]) for i in range(len(seqs))]
    SMAX = max(seqs)
    NTMAX = SMAX // 128

    def din(name, shape, dt=F32):
        return nc.dram_tensor(name, list(shape), dt, kind="ExternalInput").ap()

    x_d = din("x", [TOK, D])
    y_d = nc.dram_tensor("y", [TOK, D], F32, kind="ExternalOutput").ap()
    norm1_g = din("norm1_g", [D]); w_in = din("w_in", [D, 2048])
    qg_d = din("q_norm_g", [64]); kg_d = din("k_norm_g", [64])
    lq1 = din("lambda_q1", [64]); lk1 = din("lambda_k1", [64])
    lq2 = din("lambda_q2", [64]); lk2 = din("lambda_k2", [64])
    subln_g = din("subln_g", [128]); w_pool = din("w_pool", [4, 128, 128])
    pool_scale = din("pool_scale", [512]); w_out = din("w_out", [D, D])
    norm2_g = din("norm2_g", [D]); w_up = din("w_up", [D, 2 * DFF])
    conv_w = din("conv_w", [3, 2 * DFF]); conv_b = din("conv_b", [2 * DFF])
    w_down = din("w_down", [DFF, D])
    bands_d = din("bands", [20, 128, 128]); invf_d = din("invf", [32])

    def scratch(name, shape):
        return nc.dram_tensor(name, list(shape), BF16, kind="Internal").ap()

    QT_d = scratch("QT", [4, 128, TOK]); KT_d = scratch("KT", [4, 128, TOK])
    V_d = scratch("Vs", [4, 128, TOK // 128, 129])
    MT_d = scratch("MT", [8, 128, TOK])

    S = Sched()
    op = S.op
    dbuf = {}

    def DB(name, blk):
        k = (name, blk)
        if k not in dbuf:
            dbuf[k] = Buf("%s%d" % (name, blk))
        return dbuf[k]

    out_bufs = []

    with ExitStack() as top:
        def T(st, name, shape, dt):
            return st.enter_context(nc.sbuf_tensor(name, list(shape), dt))

        def P(st, name, shape, dt=F32):
            return st.enter_context(nc.psum_tensor(name, list(shape), dt))

        ident = T(top, "ident", [128, 128], BF16); identf = T(top, "identf", [128, 128], F32)
        neghalf = T(top, "neghalf", [128, 16], F32)
        neglam = T(top, "neglam", [128, 1], F32)
        lamt = T(top, "lamt", [128, 4, 64], F32); lams = T(top, "lams", [128, 2], F32)
        b_ident = Buf("ident"); b_nh = Buf("nh"); b_lam = Buf("lam")
        op("pool", lambda e: e.memset(identf[:], 1.0), writes=[b_ident])
        op("pool", lambda e: e.affine_select(out=identf[:], in_=identf[:], pattern=[[-1, 128]],
                                             compare_op=ALU.is_equal, fill=0.0, base=0,
                                             channel_multiplier=1), reads=[b_ident], writes=[b_ident])
        op("pool", lambda e: e.tensor_copy(out=ident[:], in_=identf[:]), reads=[b_ident], writes=[b_ident])
        op("pool", lambda e: e.memset(neghalf[:], -0.5), writes=[b_nh])
        for i, v in enumerate((lq1, lk1, lq2, lk2)):
            op("sp", lambda e, i=i, v=v: e.dma_start(out=lamt[:, i, :], in_=v.partition_broadcast(128)),
               writes=[b_lam], dma=b_lam)
        op("dve", lambda e: e.tensor_tensor(out=lamt[:, 0, :], in0=lamt[:, 0, :], in1=lamt[:, 1, :], op=ALU.mult),
           reads=[b_lam], writes=[b_lam])
        op("dve", lambda e: e.tensor_tensor(out=lamt[:, 2, :], in0=lamt[:, 2, :], in1=lamt[:, 3, :], op=ALU.mult),
           reads=[b_lam], writes=[b_lam])
        op("dve", lambda e: e.tensor_reduce(out=lams[:, 0:1], in_=lamt[:, 0, :], axis=AX.X, op=ALU.add),
           reads=[b_lam], writes=[b_lam])
        op("dve", lambda e: e.tensor_reduce(out=lams[:, 1:2], in_=lamt[:, 2, :], axis=AX.X, op=ALU.add),
           reads=[b_lam], writes=[b_lam])
        op("act", lambda e: e.activation(out=lams[:], in_=lams[:], func=AF.Exp), reads=[b_lam], writes=[b_lam])
        op("dve", lambda e: e.scalar_tensor_tensor(out=neglam[:], in0=lams[:, 1:2], scalar=-LAMBDA_INIT,
                                                   in1=lams[:, 0:1], op0=ALU.add, op1=ALU.subtract),
           reads=[b_lam], writes=[b_lam])
        b_ident.const = True; b_nh.const = True; b_lam.const = True

        def rsqrt_ops(eng_tag, dst, src, n, scale, rb, wb):
            op("dve", lambda e: e.tensor_scalar(out=dst, in0=src, scalar1=scale, scalar2=EPS,
                                                op0=ALU.mult, op1=ALU.add), reads=rb, writes=wb)
            op("pool", lambda e: e.tensor_tensor(out=dst, in0=dst, in1=neghalf[:, 0:n], op=ALU.pow),
               reads=wb + [b_nh], writes=wb)

        with ExitStack() as p1:
            if _STOP < 0.2:
                S.muted = True
            w_in_bf = T(p1, "w_in_bf", [128, 8, 2048], BF16)
            g1 = T(p1, "g1", [128, 8], F32)
            Cq = T(p1, "Cq", [128, NTMAX, 64], F32); Sq = T(p1, "Sq", [128, NTMAX, 64], F32)
            Ck = T(p1, "Ck", [128, NTMAX, 64], F32); Sk = T(p1, "Sk", [128, NTMAX, 64], F32)
            wpl = T(p1, "wpl", [128, 4, 128], BF16)
            psc = T(p1, "psc", [128, 4], F32)
            band = T(p1, "band", [128, 20, 128], BF16)
            pw1 = ExitStack()
            wst = [T(pw1, "wst%d" % i, [128, 2048], F32) for i in range(2)]
            wplf = T(pw1, "wplf", [128, 4, 128], F32)
            bandf = T(pw1, "bandf", [128, 20, 128], F32)
            b_win = Buf("w_in"); b_wst = [Buf("wst0"), Buf("wst1")]; b_g1 = Buf("g1")
            b_tab = Buf("tab"); b_wpl = Buf("wpl"); b_band = Buf("band")
            op("sp", lambda e: e.dma_start(out=g1[:], in_=norm1_g.rearrange("(k p) -> p k", p=128), allow_slow_non_contiguous=True),
               writes=[b_g1], dma=b_g1)
            for k in range(8):
                s_ = k % 2
                op("sp", lambda e, k=k, s_=s_: e.dma_start(out=wst[s_][:], in_=w_in[k * 128:(k + 1) * 128, :]),
                   writes=[b_wst[s_]], dma=b_wst[s_])
                op("dve", lambda e, k=k, s_=s_: e.tensor_scalar(out=w_in_bf[:, k, :], in0=wst[s_][:],
                                                                scalar1=g1[:, k:k + 1], scalar2=None, op0=ALU.mult),
                   reads=[b_wst[s_], b_g1], writes=[b_win])
            op("sp", lambda e: e.dma_start(out=wplf[:], in_=w_pool.rearrange("g c e -> c g e")),
               writes=[b_wpl], dma=b_wpl)
            op("sp", lambda e: e.dma_start(out=psc[:], in_=pool_scale.rearrange("(g e) -> e g", e=128), allow_slow_non_contiguous=True),
               writes=[b_wpl], dma=b_wpl)
            op("pool", lambda e: e.tensor_copy(out=wpl[:], in_=wplf[:]), reads=[b_wpl], writes=[b_wpl])
            op("sp", lambda e: e.dma_start(out=bandf[:], in_=bands_d.rearrange("n j i -> j n i")),
               writes=[b_band], dma=b_band)
            op("pool", lambda e: e.tensor_copy(out=band[:], in_=bandf[:]), reads=[b_band], writes=[b_band])

            if _STOP < 0.4:
                S.muted = True
            with ExitStack() as pt:
                S.limit = _LIMIT
                ang = T(pt, "ang", [128, NTMAX, 32], F32); kf = T(pt, "kf", [128, NTMAX, 32], F32)
                ki = T(pt, "ki", [128, NTMAX, 32], I32); mm = T(pt, "mm", [128, NTMAX, 32], F32)
                rc = T(pt, "rc", [128, NTMAX, 32], F32)
                sn = T(pt, "sn", [128, NTMAX, 32], F32); cs = T(pt, "cs", [128, NTMAX, 32], F32)
                invb = T(pt, "invb", [128, 32], F32); gq = T(pt, "gq", [128, 64], F32); gk = T(pt, "gk", [128, 64], F32)
                b_a = Buf("ang")
                C1 = 6.28125
                C2 = 2 * math.pi - 6.28125
                PI = math.pi
                op("sp", lambda e: e.dma_start(out=invb[:], in_=invf_d.partition_broadcast(128)), writes=[b_a], dma=b_a)
                op("sp", lambda e: e.dma_start(out=gq[:], in_=qg_d.partition_broadcast(128)), writes=[b_a], dma=b_a)
                op("sp", lambda e: e.dma_start(out=gk[:], in_=kg_d.partition_broadcast(128)), writes=[b_a], dma=b_a)
                op("pool", lambda e: e.iota(ki[:], pattern=[[128, NTMAX], [0, 32]], base=0, channel_multiplier=1),
                   reads=[b_a], writes=[b_a])
                op("pool", lambda e: e.tensor_copy(out=ang[:], in_=ki[:]), reads=[b_a], writes=[b_a])
                bc32 = lambda a: a.unsqueeze(1).to_broadcast([128, NTMAX, 32])
                dv = lambda f: op("dve", f, reads=[b_a], writes=[b_a])
                dv(lambda e: e.tensor_tensor(out=ang[:], in0=ang[:], in1=bc32(invb[:]), op=ALU.mult))
                dv(lambda e: e.tensor_scalar(out=kf[:], in0=ang[:], scalar1=1.0 / (2 * PI), scalar2=None, op0=ALU.mult))
                dv(lambda e: e.tensor_copy(out=ki[:], in_=kf[:]))
                dv(lambda e: e.tensor_copy(out=kf[:], in_=ki[:]))
                dv(lambda e: e.scalar_tensor_tensor(out=ang[:], in0=kf[:], scalar=-C1, in1=ang[:], op0=ALU.mult, op1=ALU.add))
                dv(lambda e: e.scalar_tensor_tensor(out=ang[:], in0=kf[:], scalar=-C2, in1=ang[:], op0=ALU.mult, op1=ALU.add))

                def wrap(t):
                    dv(lambda e: e.tensor_scalar(out=mm[:], in0=t[:], scalar1=PI, scalar2=-2 * PI, op0=ALU.is_gt, op1=ALU.mult))
                    dv(lambda e: e.tensor_tensor(out=t[:], in0=t[:], in1=mm[:], op=ALU.add))
                    dv(lambda e: e.tensor_scalar(out=mm[:], in0=t[:], scalar1=-PI, scalar2=2 * PI, op0=ALU.is_lt, op1=ALU.mult))
                    dv(lambda e: e.tensor_tensor(out=t[:], in0=t[:], in1=mm[:], op=ALU.add))
                wrap(ang)
                dv(lambda e: e.tensor_scalar(out=rc[:], in0=ang[:], scalar1=PI / 2, scalar2=None, op0=ALU.add))
                wrap(rc)
                op("act", lambda e: e.activation(out=sn[:], in_=ang[:], func=AF.Sin), reads=[b_a], writes=[b_a])
                op("act", lambda e: e.activation(out=cs[:], in_=rc[:], func=AF.Sin), reads=[b_a], writes=[b_a])
                for (Ct, St, g, sc) in ((Cq, Sq, gq, 0.125), (Ck, Sk, gk, 1.0)):
                    for hf in range(2):
                        lo, hi = hf * 32, hf * 32 + 32
                        olo, ohi = (1 - hf) * 32, (1 - hf) * 32 + 32
                        sgn = -sc if hf == 0 else sc
                        op("dve", lambda e, Ct=Ct, g=g, lo=lo, hi=hi, sc=sc: e.scalar_tensor_tensor(
                            out=Ct[:, :, lo:hi], in0=cs[:], scalar=sc, in1=bc32(g[:, lo:hi]), op0=ALU.mult, op1=ALU.mult),
                           reads=[b_a], writes=[b_tab])
                        op("dve", lambda e, St=St, g=g, lo=lo, hi=hi, olo=olo, ohi=ohi, sgn=sgn: e.scalar_tensor_tensor(
                            out=St[:, :, lo:hi], in0=sn[:], scalar=sgn, in1=bc32(g[:, olo:ohi]), op0=ALU.mult, op1=ALU.mult),
                           reads=[b_a], writes=[b_tab])
                S.fence()
            pw1.close()
            b_win.const = True; b_tab.const = True; b_wpl.const = True; b_band.const = True

            if _STOP < 0.6:
                S.muted = True
            NX = 4
            xs = [T(p1, "xs%d" % i, [128, D], F32) for i in range(NX)]; b_xs = [Buf() for _ in range(NX)]
            junk = T(p1, "junk", [128, D], BF16); b_junk = Buf()
            ss1 = [T(p1, "ss1_%d" % i, [128, 1], F32) for i in range(3)]; b_ss1 = [Buf(), Buf(), Buf()]
            hb = [T(p1, "hb%d" % i, [128, D], BF16) for i in range(3)]; b_hb = [Buf(), Buf(), Buf()]
            hT = [T(p1, "hT%d" % i, [128, 8, 128], BF16) for i in range(2)]; b_hT = [Buf(), Buf()]
            sq = [[T(p1, "sq%d_%d" % (j, i), [128, 8, 64], F32) for i in range(2)] for j in range(2)]
            b_sq = [[Buf(), Buf()], [Buf(), Buf()]]
            zsb = [[T(p1, "zsb%d_%d" % (j, i), [128, 512], F32) for i in range(2)] for j in range(2)]
            b_zsb = [[Buf(), Buf()], [Buf(), Buf()]]
            ta = [[T(p1, "ta%d_%d" % (j, i), [128, 8, 64], F32) for i in range(2)] for j in range(2)]
            tb = [[T(p1, "tb%d_%d" % (j, i), [128, 8, 64], F32) for i in range(2)] for j in range(2)]
            rs = [T(p1, "rs_%d" % i, [128, 2, 8], F32) for i in range(2)]
            b_ta = [[Buf(), Buf()], [Buf(), Buf()]]; b_tb = [[Buf(), Buf()], [Buf(), Buf()]]; b_rs = [Buf(), Buf()]
            qkb = [[T(p1, "qkb%d_%d" % (i, j), [128, 512], BF16) for j in range(2)] for i in range(2)]
            b_qkb = [[Buf(), Buf()], [Buf(), Buf()]]
            NPB = 4
            pb = [T(p1, "pb%d" % i, [128, 512], BF16) for i in range(NPB)]; b_pb = [Buf() for _ in range(NPB)]
            pgT = T(p1, "pgT", [128, 4, 128], BF16); b_pgT = Buf()
            qst = [[T(p1, "qst%d_%d" % (i, j), [128, 4, 512], BF16) for j in range(2)] for i in range(2)]
            b_qst = [[Buf(), Buf()], [Buf(), Buf()]]
            vst = [T(p1, "vst%d" % i, [128, 4, 4, 129], BF16) for i in range(2)]; b_vst = [Buf() for _ in range(2)]
            for i_ in range(2):
                op("pool", lambda e, i_=i_: e.memset(vst[i_][:, :, :, 128:129], 1.0), writes=[b_vst[i_]])
            ost = [T(p1, "ost%d" % i, [128, 4, 512], BF16) for i in range(2)]; b_ost = [Buf(), Buf()]
            with ExitStack() as pp:
                zps = [P(pp, "z%d" % i, [128, 512]) for i in range(4)]; b_z = [Buf(psum=True) for _ in range(4)]
                pTp = P(pp, "pTp", [128, 8, 128], BF16); b_pTp = Buf(psum=True)
                qkTp = P(pp, "qkTp", [128, 2, 4, 128], BF16); b_qkTp = Buf(psum=True)
                plp = P(pp, "plp", [128, 4, 128]); b_plp = Buf(psum=True)
                opp = P(pp, "opp", [128, 4, 128]); b_opp = Buf(psum=True)

                tiles = []
                for si, Sl in enumerate(seqs):
                    for t in range(Sl // 128):
                        tiles.append((si, t, Sl // 128, offs[si] + t * 128))
                NTT = len(tiles)

                def stage_L(i):
                    s_ = i % NX
                    g0 = tiles[i][3]
                    op("sp", lambda e: e.dma_start(out=xs[s_][:], in_=x_d[g0:g0 + 128, :]), writes=[b_xs[s_]], dma=b_xs[s_])

                def stage_Na(i):
                    s_ = i % NX; h_ = i % 3
                    op("act", lambda e: e.activation(out=junk[:], in_=xs[s_][:], func=AF.Square, accum_out=ss1[h_][:]),
                       reads=[b_xs[s_]], writes=[b_junk, b_ss1[h_]])
                    rsqrt_ops("p1", ss1[h_][:], ss1[h_][:], 1, 1.0 / D, [b_ss1[h_]], [b_ss1[h_]])

                def stage_Nb(i):
                    s_ = i % NX; h_ = i % 3
                    op("act", lambda e: e.activation(out=hb[h_][:], in_=xs[s_][:], func=AF.Copy, scale=ss1[h_][:]),
                       reads=[b_xs[s_], b_ss1[h_]], writes=[b_hb[h_]])

                def stage_T(i):
                    h_ = i % 2
                    h3 = i % 3
                    for k in range(8):
                        op("pe", lambda e, k=k: e.transpose(out=pTp[:, k, :], in_=hb[h3][:, k * 128:(k + 1) * 128], identity=ident[:]),
                           reads=[b_hb[h3], b_ident], writes=[b_pTp])
                    op("dve", lambda e: e.tensor_copy(out=hT[h_][:], in_=pTp[:]), reads=[b_pTp], writes=[b_hT[h_]])

                def stage_M(i):
                    h_ = i % 2
                    for grp in ((0, 1), (2, 3)):
                        for k in range(8):
                            for n in grp:
                                op("pe", lambda e, k=k, n=n: e.matmul(zps[n][:], lhsT=hT[h_][:, k, :],
                                                                      rhs=w_in_bf[:, k, n * 512:(n + 1) * 512],
                                                                      start=(k == 0), stop=(k == 7)),
                                   reads=[b_hT[h_], b_win], writes=[b_z[n]])

                def stage_E(i):
                    si, t, nt, g0 = tiles[i]
                    h_ = i % 2
                    for j in range(2):
                        op("act", lambda e, j=j: e.activation(out=zsb[j][h_][:], in_=zps[j][:], func=AF.Copy),
                           reads=[b_z[j]], writes=[b_zsb[j][h_]])
                    vblk = g0 // 512
                    vsub = (g0 % 512) // 128
                    vs_ = vblk % 2
                    op("act", lambda e: e.activation(out=vst[vs_][:, :, vsub, 0:128], in_=zps[2][:].rearrange("p (h e) -> p h e", e=128), func=AF.Copy),
                       reads=[b_z[2]], writes=[b_vst[vs_]])
                    if vsub == 3 or t == nt - 1:
                        c0 = vblk * 4
                        op("sp", lambda e: e.dma_start(out=V_d[:, :, c0:c0 + vsub + 1, :].rearrange("h p c e -> p h c e"),
                                                         in_=vst[vs_][:, :, 0:vsub + 1, :]),
                           reads=[b_vst[vs_]], writes=[DB("V", vblk)], dma=b_vst[vs_])
                    ps_ = i % NPB
                    op("act", lambda e: e.activation(out=pb[ps_][:], in_=zps[3][:], func=AF.Copy), reads=[b_z[3]], writes=[b_pb[ps_]])

                def stage_E2(i):
                    si, t, nt, g0 = tiles[i]
                    h_ = i % 2
                    for j in range(2):
                        Ct, St = ((Cq, Sq), (Ck, Sk))[j]
                        z = zsb[j][h_][:].rearrange("p (a b) -> p a b", b=64)
                        zb = b_zsb[j][h_]
                        A, Bq, Q = ta[j][h_], tb[j][h_], sq[j][h_]
                        op("act", lambda e, z=z, Q=Q: e.activation(out=Q[:], in_=z, func=AF.Square), reads=[zb], writes=[b_sq[j][h_]])
                        cb = Ct[:, t, :].unsqueeze(1).to_broadcast([128, 8, 64])
                        op("dve", lambda e, z=z, A=A, cb=cb: e.tensor_tensor(out=A[:], in0=z, in1=cb, op=ALU.mult),
                           reads=[zb, b_tab], writes=[b_ta[j][h_]])
                        for hf in range(2):
                            lo, hi = hf * 32, hf * 32 + 32
                            olo, ohi = (1 - hf) * 32, (1 - hf) * 32 + 32
                            sb = St[:, t, lo:hi].unsqueeze(1).to_broadcast([128, 8, 32])
                            op("dve", lambda e, z=z, Bq=Bq, sb=sb, lo=lo, hi=hi, olo=olo, ohi=ohi:
                               e.tensor_tensor(out=Bq[:, :, lo:hi], in0=z[:, :, olo:ohi], in1=sb, op=ALU.mult),
                               reads=[zb, b_tab], writes=[b_tb[j][h_]])
                    R = rs[h_]
                    for j in range(2):
                        Q = sq[j][h_]
                        op("dve", lambda e, Q=Q, j=j: e.tensor_reduce(out=R[:, j, :], in_=Q[:], axis=AX.X, op=ALU.add),
                           reads=[b_sq[j][h_]], writes=[b_rs[h_]])
                    Rf = R[:].rearrange("p a b -> p (a b)")
                    op("dve", lambda e: e.tensor_scalar(out=Rf, in0=Rf, scalar1=1.0 / 64, scalar2=EPS, op0=ALU.mult, op1=ALU.add),
                       reads=[b_rs[h_]], writes=[b_rs[h_]])
                    op("act", lambda e: e.activation(out=Rf, in_=Rf, func=AF.Ln), reads=[b_rs[h_]], writes=[b_rs[h_]])
                    op("act", lambda e: e.activation(out=Rf, in_=Rf, func=AF.Exp, scale=-0.5), reads=[b_rs[h_]], writes=[b_rs[h_]])
                    for j in range(2):
                        A, Bq = ta[j][h_], tb[j][h_]
                        op("dve", lambda e, A=A, Bq=Bq: e.tensor_tensor(out=A[:], in0=A[:], in1=Bq[:], op=ALU.add),
                           reads=[b_ta[j][h_], b_tb[j][h_]], writes=[b_ta[j][h_]])
                    for j in range(2):
                        A = ta[j][h_]
                        rb = R[:, j, :].unsqueeze(2).to_broadcast([128, 8, 64])
                        dst = qkb[j][h_][:].rearrange("p (a b) -> p a b", b=64)
                        op("dve", lambda e, A=A, rb=rb, dst=dst: e.tensor_tensor(out=dst, in0=A[:], in1=rb, op=ALU.mult),
                           reads=[b_ta[j][h_], b_rs[h_]], writes=[b_qkb[j][h_]])

                def stage_Q(i):
                    si, t, nt, g0 = tiles[i]
                    h_ = i % 2
                    blk = g0 // 512
                    sub = (g0 % 512) // 128
                    st_ = blk % 2
                    for j in range(2):
                        for hh in range(4):
                            op("pe", lambda e, j=j, hh=hh: e.transpose(out=qkTp[:, j, hh, :], in_=qkb[j][h_][:, hh * 128:(hh + 1) * 128],
                                                                     identity=ident[:]),
                               reads=[b_qkb[j][h_], b_ident], writes=[b_qkTp])
                    for j in range(2):
                        eng = "act"
                        if eng == "dve":
                            op("dve", lambda e, j=j: e.tensor_copy(out=qst[j][st_][:, :, sub * 128:(sub + 1) * 128], in_=qkTp[:, j, :, :]),
                               reads=[b_qkTp], writes=[b_qst[j][st_]])
                        else:
                            op("act", lambda e, j=j: e.activation(out=qst[j][st_][:, :, sub * 128:(sub + 1) * 128], in_=qkTp[:, j, :, :], func=AF.Copy),
                               reads=[b_qkTp], writes=[b_qst[j][st_]])
                    last_in_blk = (sub == 3) or (t == nt - 1)
                    if last_in_blk:
                        n = (sub + 1) * 128
                        b0 = blk * 512
                        for j, dst in enumerate((QT_d, KT_d)):
                            op("sp", lambda e, j=j, dst=dst: e.dma_start(out=dst[:, :, b0:b0 + n].rearrange("h p t -> p h t"),
                                                                          in_=qst[j][st_][:, :, 0:n]),
                               reads=[b_qst[j][st_]], writes=[DB("QT" if j == 0 else "KT", blk)], dma=b_qst[j][st_])

                def stage_P(i):
                    si, t, nt, g0 = tiles[i]
                    blk = g0 // 512
                    sub = (g0 % 512) // 128
                    st_ = blk % 2
                    srcs = []
                    if t > 0:
                        srcs.append((i - 1, 0))
                    srcs.append((i, 3 if t == 0 else (4 if t == nt - 1 else 1)))
                    if t < nt - 1:
                        srcs.append((i + 1, 2))
                    for g in range(4):
                        for n_, (ii, kind) in enumerate(srcs):
                            op("pe", lambda e, g=g, ii=ii, kind=kind, n_=n_: e.matmul(
                                plp[:, g, :], lhsT=pb[ii % NPB][:, g * 128:(g + 1) * 128], rhs=band[:, g * 5 + kind, :],
                                start=(n_ == 0), stop=(n_ == len(srcs) - 1)),
                               reads=[b_pb[ii % NPB], b_band], writes=[b_plp])
                    op("dve", lambda e: e.tensor_copy(out=pgT[:], in_=plp[:]), reads=[b_plp], writes=[b_pgT])
                    for g in range(4):
                        op("pe", lambda e, g=g: e.matmul(opp[:, g, :], lhsT=wpl[:, g, :], rhs=pgT[:, g, :], start=True, stop=True),
                           reads=[b_pgT, b_wpl], writes=[b_opp])
                    for g in range(4):
                        op("dve", lambda e, g=g: e.tensor_scalar(out=ost[st_][:, g, sub * 128:(sub + 1) * 128], in0=opp[:, g, :],
                                                                 scalar1=psc[:, g:g + 1], scalar2=None, op0=ALU.mult),
                           reads=[b_opp, b_wpl], writes=[b_ost[st_]])
                    if sub == 3 or t == nt - 1:
                        n = (sub + 1) * 128
                        b0 = blk * 512
                        op("sp", lambda e: e.dma_start(out=MT_d[4:8, :, b0:b0 + n].rearrange("h p t -> p h t"), in_=ost[st_][:, :, 0:n]),
                           reads=[b_ost[st_]], writes=[DB("MTP", blk)], dma=b_ost[st_])

                for i_ in range(min(3, NTT)):
                    stage_L(i_)
                stage_Na(0); stage_Nb(0)
                if NTT > 1:
                    stage_Na(1); stage_Nb(1)
                stage_T(0)
                for i in range(NTT + 2):
                    if i + 3 < NTT:
                        stage_L(i + 3)
                    if i + 2 < NTT:
                        stage_Na(i + 2)
                    if i < NTT:
                        stage_M(i)
                    if i + 1 < NTT:
                        stage_T(i + 1)
                    if 0 <= i - 2 < NTT:
                        stage_Q(i - 2)
                        stage_P(i - 2)
                    if i < NTT:
                        stage_E(i)
                    if i + 2 < NTT:
                        stage_Nb(i + 2)
                    if i < NTT:
                        stage_E2(i)
                S.fence()

        with ExitStack() as p2:
            if _STOP < 2:
                S.muted = True
            NCMAX = SMAX // 128
            KTs = [T(p2, "KTs%d" % i, [128, SMAX], BF16) for i in range(2)]; b_KT = [Buf(), Buf()]
            Vx = [T(p2, "Vx%d" % i, [128, NCMAX, 129], BF16) for i in range(2)]; b_V = [Buf(), Buf()]
            NQ = 3
            Qs = [T(p2, "Qs%d" % i, [128, 512], BF16) for i in range(NQ)]; b_Q = [Buf() for _ in range(NQ)]
            NE = 3
            Es = [T(p2, "Es%d" % i, [128, 1024], BF16) for i in range(NE)]; b_E = [Buf() for _ in range(NE)]
            Oc = T(p2, "Oc", [128, 8, 129], F32); b_Oc = Buf()
            rr = [T(p2, "rr%d" % i, [128, 2], F32) for i in range(2)]; b_rr = [Buf(), Buf()]
            t0s = [T(p2, "t0s%d" % i, [128, 128], F32) for i in range(2)]; b_t0 = [Buf(), Buf()]
            osq = [T(p2, "osq%d" % i, [128, 128], F32) for i in range(2)]; b_osq = [Buf(), Buf()]
            oss = [T(p2, "oss%d" % i, [128, 1], F32) for i in range(2)]; b_oss = [Buf(), Buf()]
            onb = [T(p2, "onb%d" % i, [128, 128], BF16) for i in range(4)]; b_onb = [Buf() for _ in range(4)]
            oat = [T(p2, "oat%d" % i, [128, 512], BF16) for i in range(2)]; b_oat = [Buf(), Buf()]
            with ExitStack() as pp:
                Sp = [P(pp, "Sp%d" % i, [128, 1024]) for i in range(2)]; b_Sp = [Buf(psum=True), Buf(psum=True)]
                Op_ = [P(pp, "Op%d" % i, [128, 512]) for i in range(3)]; b_Op = [Buf(psum=True), Buf(psum=True), Buf(psum=True)]
                oTp = P(pp, "oTp", [128, 8, 128], BF16); b_oTp = Buf(psum=True)

                def acc(a):
                    return Op_[a // 3][:, (a % 3) * 129:(a % 3 + 1) * 129], b_Op[a // 3]

                heads = [(si, h) for si in range(len(seqs)) for h in range(4)]

                def load_head(n):
                    si, h = heads[n]
                    Sl = seqs[si]; o0 = offs[si]; s_ = n % 2
                    rd = [DB("KT", (o0 + b) // 512) for b in range(0, Sl, 512)]
                    op("sp", lambda e: e.dma_start(out=KTs[s_][:, 0:Sl], in_=KT_d[h, :, o0:o0 + Sl]),
                       reads=rd, writes=[b_KT[s_]], dma=b_KT[s_])
                    rd = [DB("V", (o0 + b) // 512) for b in range(0, Sl, 512)]
                    op("sp", lambda e: e.dma_start(out=Vx[s_][:, 0:Sl // 128, :], in_=V_d[h, :, o0 // 128:(o0 + Sl) // 128, :]),
                       reads=rd, writes=[b_V[s_]], dma=b_V[s_])

                qtiles = []
                for n, (si, h) in enumerate(heads):
                    Sl = seqs[si]
                    QTL = min(512, Sl)
                    for q0 in range(0, Sl, QTL):
                        qtiles.append((n, si, h, q0, QTL))

                def load_q(qi):
                    n, si, h, q0, QTL = qtiles[qi]
                    g0 = offs[si] + q0
                    s_ = qi % NQ
                    op("sp", lambda e: e.dma_start(out=Qs[s_][:, 0:QTL], in_=QT_d[h, :, g0:g0 + QTL]),
                       reads=[DB("QT", g0 // 512)], writes=[b_Q[s_]], dma=b_Q[s_])

                ectr = [0]

                def S_step(qi, kc, b):
                    n, si, h, q0, QTL = qtiles[qi]
                    s_ = n % 2; qs_ = qi % NQ
                    op("pe", lambda e: e.matmul(Sp[b][:, 0:QTL], lhsT=KTs[s_][0:64, kc * 128:(kc + 1) * 128],
                                                rhs=Qs[qs_][0:64, 0:QTL], start=True, stop=True, tile_position=(0, 0)),
                       reads=[b_KT[s_], b_Q[qs_]], writes=[b_Sp[b]])
                    op("pe", lambda e: e.matmul(Sp[b][:, 512:512 + QTL], lhsT=KTs[s_][64:128, kc * 128:(kc + 1) * 128],
                                                rhs=Qs[qs_][64:128, 0:QTL], start=True, stop=True, tile_position=(64, 0)),
                       reads=[b_KT[s_], b_Q[qs_]], writes=[b_Sp[b]])

                def E_step(qi, kc, b):
                    n, si, h, q0, QTL = qtiles[qi]
                    es_ = ectr[0] % NE
                    ectr[0] += 1
                    if QTL == 512:
                        op("act", lambda e: e.activation(out=Es[es_][:], in_=Sp[b][:], func=AF.Exp),
                           reads=[b_Sp[b]], writes=[b_E[es_]])
                    else:
                        for m in range(2):
                            op("act", lambda e, m=m: e.activation(out=Es[es_][:, m * 512:m * 512 + QTL],
                                                                   in_=Sp[b][:, m * 512:m * 512 + QTL], func=AF.Exp),
                               reads=[b_Sp[b]], writes=[b_E[es_]])
                    return es_

                def PV_step(qi, kc, es_):
                    n, si, h, q0, QTL = qtiles[qi]
                    s_ = n % 2
                    NC_ = seqs[si] // 128
                    for qq in range(QTL // 128):
                        for m in range(2):
                            ap_, bb = acc(qq * 2 + m)
                            first = (kc == 0 and (qq * 2 + m) % 3 == 0)
                            op("pe", lambda e, qq=qq, m=m, ap_=ap_, first=first: e.matmul(
                                ap_, lhsT=Es[es_][:, m * 512 + qq * 128:m * 512 + (qq + 1) * 128],
                                rhs=Vx[s_][:, kc, :], start=first, stop=(kc == NC_ - 1), skip_group_check=True),
                               reads=[b_E[es_], b_V[s_]], writes=[bb])

                def finalize_a(qi):
                    n, si, h, q0, QTL = qtiles[qi]
                    NQQ = QTL // 128
                    for bnk in range(3):
                        na = min(3, NQQ * 2 - bnk * 3)
                        if na <= 0:
                            continue
                        op("dve", lambda e, bnk=bnk, na=na: e.tensor_copy(out=Oc[:, bnk * 3:bnk * 3 + na, :], in_=Op_[bnk][:, 0:na * 129].rearrange("p (a b) -> p a b", b=129)),
                           reads=[b_Op[bnk]], writes=[b_Oc])

                def finalize_b(qi):
                    n, si, h, q0, QTL = qtiles[qi]
                    NQQ = QTL // 128
                    g0 = offs[si] + q0
                    o_ = qi % 2
                    for qq in range(NQQ):
                        p_ = qq % 2
                        op("dve", lambda e, qq=qq, p_=p_: e.reciprocal(out=rr[p_][:], in_=Oc[:, 2 * qq:2 * qq + 2, 128]),
                           reads=[b_Oc], writes=[b_rr[p_]])
                        op("dve", lambda e, p_=p_: e.tensor_scalar(out=rr[p_][:, 1:2], in0=rr[p_][:, 1:2], scalar1=neglam[:, 0:1],
                                                                  scalar2=None, op0=ALU.mult),
                           reads=[b_rr[p_], b_lam], writes=[b_rr[p_]])
                        op("dve", lambda e, qq=qq, p_=p_: e.tensor_scalar(out=t0s[p_][:], in0=Oc[:, 2 * qq, 0:128], scalar1=rr[p_][:, 0:1],
                                                                         scalar2=None, op0=ALU.mult),
                           reads=[b_Oc, b_rr[p_]], writes=[b_t0[p_]])
                        op("dve", lambda e, qq=qq, p_=p_: e.scalar_tensor_tensor(out=t0s[p_][:], in0=Oc[:, 2 * qq + 1, 0:128],
                                                                                scalar=rr[p_][:, 1:2], in1=t0s[p_][:],
                                                                                op0=ALU.mult, op1=ALU.add),
                           reads=[b_Oc, b_rr[p_], b_t0[p_]], writes=[b_t0[p_]])
                        op("pool", lambda e, p_=p_: e.tensor_tensor(out=osq[p_][:], in0=t0s[p_][:], in1=t0s[p_][:], op=ALU.mult),
                           reads=[b_t0[p_]], writes=[b_osq[p_]])
                        op("dve", lambda e, p_=p_: e.tensor_reduce(out=oss[p_][:], in_=osq[p_][:], axis=AX.X, op=ALU.add),
                           reads=[b_osq[p_]], writes=[b_oss[p_]])
                        rsqrt_ops("p2", oss[p_][:], oss[p_][:], 1, 1.0 / 128, [b_oss[p_]], [b_oss[p_]])
                        op("dve", lambda e, p_=p_, qq=qq: e.tensor_scalar(out=onb[qq][:], in0=t0s[p_][:], scalar1=oss[p_][:, 0:1],
                                                                         scalar2=None, op0=ALU.mult),
                           reads=[b_t0[p_], b_oss[p_]], writes=[b_onb[qq]])

                def finalize_c(qi):
                    n, si, h, q0, QTL = qtiles[qi]
                    NQQ = QTL // 128
                    g0 = offs[si] + q0
                    o_ = qi % 2
                    for qq in range(NQQ):
                        op("pe", lambda e, qq=qq: e.transpose(out=oTp[:, qq, :], in_=onb[qq][:], identity=ident[:]),
                           reads=[b_onb[qq], b_ident], writes=[b_oTp])
                    op("dve", lambda e: e.tensor_copy(out=oat[o_][:, 0:QTL], in_=oTp[:, 0:NQQ, :].rearrange("p a b -> p (a b)")),
                       reads=[b_oTp], writes=[b_oat[o_]])
                    op("sp", lambda e: e.dma_start(out=MT_d[h, :, g0:g0 + QTL], in_=oat[o_][:, 0:QTL]),
                       reads=[b_oat[o_]], writes=[DB("MTA%d" % h, g0 // 512)], dma=b_oat[o_])

                G = []
                for qi, (n, si, h, q0, QTL) in enumerate(qtiles):
                    for kc in range(seqs[si] // 128):
                        G.append((qi, kc))
                load_head(0)
                load_q(0)
                if len(qtiles) > 1:
                    load_q(1)
                if len(heads) > 1:
                    load_head(1)
                cur_head = 0
                for g_ in range(min(2, len(G))):
                    S_step(G[g_][0], G[g_][1], g_ % 2)
                for g_, (qi, kc) in enumerate(G):
                    n, si, h, q0, QTL = qtiles[qi]
                    NC_ = seqs[si] // 128
                    if kc == 0:
                        if n != cur_head:
                            cur_head = n
                            if n + 1 < len(heads):
                                load_head(n + 1)
                        if qi + 2 < len(qtiles):
                            load_q(qi + 2)
                        if qi > 0:
                            finalize_a(qi - 1)
                    es_ = E_step(qi, kc, g_ % 2)
                    if g_ + 2 < len(G):
                        S_step(G[g_ + 2][0], G[g_ + 2][1], g_ % 2)
                    PV_step(qi, kc, es_)
                    if qi > 0 and kc == min(2, NC_ - 1):
                        finalize_b(qi - 1)
                    if qi > 0 and kc == min(26, NC_ - 1):
                        finalize_c(qi - 1)
                finalize_a(len(qtiles) - 1)
                finalize_b(len(qtiles) - 1)
                finalize_c(len(qtiles) - 1)
                S.fence()

        with ExitStack() as p3:
            if _STOP < 3:
                S.muted = True
            w_out_bf = T(p3, "w_out_bf", [128, 8, D], BF16)
            w_up_bf = T(p3, "w_up_bf", [128, 8, 2 * DFF], BF16)
            w_dn_bf = T(p3, "w_dn_bf", [128, NFC, D], BF16)
            g2 = T(p3, "g2", [128, 8], F32); sg = T(p3, "sg", [128, 1], F32)
            cw = T(p3, "cw", [128, 3, 2 * NFC], F32); cbias = T(p3, "cbias", [128, 2 * NFC], F32)
            b_w3 = Buf("w3"); b_g2 = Buf("g2")
            op("sp", lambda e: e.dma_start(out=g2[:], in_=norm2_g.rearrange("(k p) -> p k", p=128), allow_slow_non_contiguous=True), writes=[b_g2], dma=b_g2)
            op("sp", lambda e: e.dma_start(out=sg[:], in_=subln_g.rearrange("(p o) -> p o", o=1)), writes=[b_g2], dma=b_g2)
            pq0 = ExitStack()
            cwT = T(pq0, "cwT", [2 * NFC, 4, 128], F32); b_cwT = Buf("cwT")
            op("sp", lambda e: e.dma_start(out=cwT[:, 0:3, :], in_=conv_w.rearrange("k (c p) -> c k p", p=128)), writes=[b_cwT], dma=b_cwT)
            op("sp", lambda e: e.dma_start(out=cwT[:, 3, :], in_=conv_b.rearrange("(c p) -> c p", p=128)), writes=[b_cwT], dma=b_cwT)
            with ExitStack() as pq:
                cwp = P(pq, "cwp", [128, 4, 128]); b_cwp = Buf("cwp", psum=True)
                for k in range(4):
                    op("pe", lambda e, k=k: e.transpose(out=cwp[:, k, 0:2 * NFC], in_=cwT[:, k, :], identity=identf[0:2 * NFC, 0:2 * NFC]),
                       reads=[b_cwT, b_ident], writes=[b_cwp])
                op("dve", lambda e: e.tensor_copy(out=cw[:], in_=cwp[:, 0:3, 0:2 * NFC]), reads=[b_cwp], writes=[b_g2])
                op("dve", lambda e: e.tensor_copy(out=cbias[:], in_=cwp[:, 3, 0:2 * NFC]), reads=[b_cwp], writes=[b_g2])
            op("dve", lambda e: e.tensor_scalar(out=sg[:], in0=sg[:], scalar1=1.0 - LAMBDA_INIT, scalar2=None, op0=ALU.mult),
               reads=[b_g2], writes=[b_g2])
            with ExitStack() as pw:
                WS = 1408
                NWS = 8
                wst3 = [T(pw, "wst3_%d" % i, [128, WS], F32) for i in range(NWS)]; b_wst3 = [Buf() for _ in range(NWS)]
                cnt = [0]

                def prep(dst_ap_fn, src_ap, ncols, scal):
                    for c0 in range(0, ncols, WS):
                        n = min(WS, ncols - c0)
                        s_ = cnt[0] % NWS
                        eng = ("dve", "act")[cnt[0] % 2]
                        cnt[0] += 1
                        op("sp", lambda e, c0=c0, n=n, s_=s_: e.dma_start(out=wst3[s_][:, 0:n], in_=src_ap[:, c0:c0 + n]),
                           writes=[b_wst3[s_]], dma=b_wst3[s_])
                        if eng == "act":
                            op("act", lambda e, c0=c0, n=n, s_=s_: e.activation(out=dst_ap_fn(c0, n), in_=wst3[s_][:, 0:n], func=AF.Copy,
                                                                              scale=(scal if scal is not None else 1.0)),
                               reads=[b_wst3[s_], b_g2], writes=[b_w3])
                        elif scal is None:
                            op(eng, lambda e, c0=c0, n=n, s_=s_: e.tensor_copy(out=dst_ap_fn(c0, n), in_=wst3[s_][:, 0:n]),
                               reads=[b_wst3[s_]], writes=[b_w3])
                        else:
                            op(eng, lambda e, c0=c0, n=n, s_=s_: e.tensor_scalar(out=dst_ap_fn(c0, n), in0=wst3[s_][:, 0:n],
                                                                               scalar1=scal, scalar2=None, op0=ALU.mult),
                               reads=[b_wst3[s_], b_g2], writes=[b_w3])
                for k in range(8):
                    prep(lambda c0, n, k=k: w_out_bf[:, k, c0:c0 + n], w_out[k * 128:(k + 1) * 128, :], D,
                         sg[:, 0:1] if k < 4 else None)
                for k in range(8):
                    prep(lambda c0, n, k=k: w_up_bf[:, k, c0:c0 + n], w_up[k * 128:(k + 1) * 128, :], 2 * DFF, g2[:, k:k + 1])
                for k in range(NFC):
                    prep(lambda c0, n, k=k: w_dn_bf[:, k, c0:c0 + n], w_down[k * 128:(k + 1) * 128, :], D, None)
                S.fence()
            pq0.close()
            b_w3.const = True; b_g2.const = True

            TL = 256
            NG = 5
            x1 = [T(p3, "x1_%d" % i, [128, D], F32) for i in range(NG)]; b_x1 = [Buf() for _ in range(NG)]
            mixT = T(p3, "mixT", [128, 8, TL], BF16); b_mix = Buf()
            h2T = [T(p3, "h2T%d" % i, [128, 8, TL + 2], BF16) for i in range(2)]; b_h2T = [Buf(), Buf()]
            actT = T(p3, "actT", [128, NFC, TL], BF16); b_actc = [Buf() for _ in range(NFC)]
            ss3 = [T(p3, "ss3_%d" % i, [128, 1], F32) for i in range(2)]; b_ss3 = [Buf(), Buf()]
            hb3 = [T(p3, "hb3_%d" % i, [128, D], BF16) for i in range(2)]; b_hb3 = [Buf(), Buf()]
            cg = [T(p3, "cg%d" % i, [128, TL], F32) for i in range(2)]; b_cg = [Buf(), Buf()]
            cv = [T(p3, "cv%d" % i, [128, TL], F32) for i in range(3)]; b_cv = [Buf(), Buf(), Buf()]
            sgt = [T(p3, "sgt%d" % i, [128, TL], F32) for i in range(2)]; b_sgt = [Buf(), Buf()]
            with ExitStack() as pp:
                NUP = 5
                NWP = 2
                Up = [P(pp, "Up%d" % i, [128, 512]) for i in range(NUP)]; b_Up = [Buf(psum=True) for _ in range(NUP)]
                Wp = [P(pp, "Wp%d" % i, [128, 512]) for i in range(NWP)]; b_Wp = [Buf(psum=True) for _ in range(NWP)]
                uctr = [0]
                tTp = P(pp, "tTp", [128, 8, 128], BF16); b_tTp = Buf(psum=True)
                wctr = [0]
                t3 = []
                for si, Sl in enumerate(seqs):
                    for t in range(Sl // TL):
                        t3.append((si, t, Sl // TL, offs[si] + t * TL))
                N3 = len(t3)
                gctr = {}

                def gslot(i, g):
                    return (2 * i + g) % NG

                def load_x(i, g):
                    if i >= N3:
                        return
                    si, t, nt, g0 = t3[i]
                    s_ = gslot(i, g)
                    op("sp", lambda e: e.dma_start(out=x1[s_][:], in_=x_d[g0 + g * 128:g0 + (g + 1) * 128, :]),
                       writes=[b_x1[s_]], dma=b_x1[s_])

                def load_mix(i):
                    if i >= N3:
                        return
                    si, t, nt, g0 = t3[i]
                    rd = [DB("MTP", g0 // 512)] + [DB("MTA%d" % h, g0 // 512) for h in range(4)]
                    op("sp", lambda e: e.dma_start(out=mixT[:], in_=MT_d[:, :, g0:g0 + TL].rearrange("c p t -> p c t")),
                       reads=rd, writes=[b_mix], dma=b_mix)

                def A_T(i, g):
                    hs = i % 2
                    p_ = g
                    for k in range(8):
                        op("pe", lambda e, k=k, p_=p_: e.transpose(out=tTp[:, k, :], in_=hb3[p_][:, k * 128:(k + 1) * 128], identity=ident[:]),
                           reads=[b_hb3[p_], b_ident], writes=[b_tTp])
                    op("dve", lambda e, g=g: e.tensor_copy(out=h2T[hs][:, :, 1 + g * 128:1 + (g + 1) * 128], in_=tTp[:]),
                       reads=[b_tTp], writes=[b_h2T[hs]])

                def A1(i, g):
                    if i >= N3:
                        return
                    s_ = gslot(i, g)
                    p_ = g
                    for half in range(2):
                        w_ = wctr[0] % NWP
                        wctr[0] += 1
                        for k in range(8):
                            op("pe", lambda e, k=k, half=half, w_=w_: e.matmul(
                                Wp[w_][:], lhsT=mixT[:, k, g * 128:(g + 1) * 128], rhs=w_out_bf[:, k, half * 512:(half + 1) * 512],
                                start=(k == 0), stop=(k == 7)), reads=[b_mix, b_w3], writes=[b_Wp[w_]])
                        op("dve", lambda e, half=half, w_=w_: e.tensor_tensor(
                            out=x1[s_][:, half * 512:(half + 1) * 512], in0=Wp[w_][:], in1=x1[s_][:, half * 512:(half + 1) * 512], op=ALU.add),
                           reads=[b_Wp[w_], b_x1[s_]], writes=[b_x1[s_]])
                    op("act", lambda e: e.activation(out=hb3[p_][:], in_=x1[s_][:], func=AF.Square, accum_out=ss3[p_][:]),
                       reads=[b_x1[s_]], writes=[b_hb3[p_], b_ss3[p_]])
                    rsqrt_ops("p3", ss3[p_][:], ss3[p_][:], 1, 1.0 / D, [b_ss3[p_]], [b_ss3[p_]])
                    op("act", lambda e: e.activation(out=hb3[p_][:], in_=x1[s_][:], func=AF.Copy, scale=ss3[p_][:]),
                       reads=[b_x1[s_], b_ss3[p_]], writes=[b_hb3[p_]])

                def A_halo(i):
                    si, t, nt, g0 = t3[i]
                    hs = i % 2
                    if t == 0:
                        op("dve", lambda e: e.memset(h2T[hs][:, :, 0:1], 0.0), writes=[b_h2T[hs]])
                    else:
                        op("dve", lambda e: e.tensor_copy(out=h2T[hs][:, :, 0:1], in_=h2T[1 - hs][:, :, TL:TL + 1]),
                           reads=[b_h2T[1 - hs]], writes=[b_h2T[hs]])
                        op("dve", lambda e: e.tensor_copy(out=h2T[1 - hs][:, :, TL + 1:TL + 2], in_=h2T[hs][:, :, 1:2]),
                           reads=[b_h2T[hs]], writes=[b_h2T[1 - hs]])
                    if t == nt - 1:
                        op("dve", lambda e: e.memset(h2T[hs][:, :, TL + 1:TL + 2], 0.0), writes=[b_h2T[hs]])

                def stage_Bup(i, tail=None):
                    hs = i % 2
                    NW = TL + 2
                    pend = []
                    for j in range(NFC + 1):
                        if j == NFC:
                            pend.pop(0)()
                            break
                        if j == 4 and tail is not None:
                            tail()
                        ubs = (uctr[0] % NUP, (uctr[0] + 1) % NUP)
                        uctr[0] += 2
                        for which, c in ((0, j), (1, j + NFC)):
                            ub = ubs[which]
                            for k in range(8):
                                op("pe", lambda e, k=k, c=c, ub=ub: e.matmul(
                                    Up[ub][:, 0:NW], lhsT=w_up_bf[:, k, c * 128:(c + 1) * 128], rhs=h2T[hs][:, k, :],
                                    start=(k == 0), stop=(k == 7)), reads=[b_h2T[hs], b_w3], writes=[b_Up[ub]])
                        p_ = j % 2
                        pv_ = j % 3
                        gv = ((0, j, cg, b_cg, p_), (1, j + NFC, cv, b_cv, pv_))
                        for which, c, dstl, dbl, sl_ in gv:
                            ub = ubs[which]
                            dst = dstl[sl_]
                            op("act", lambda e, c=c, ub=ub, dst=dst: e.activation(
                                out=dst[:], in_=Up[ub][:, 1:TL + 1], func=AF.Identity, scale=cw[:, 1, c:c + 1], bias=cbias[:, c:c + 1]),
                               reads=[b_Up[ub], b_g2], writes=[dbl[sl_]])
                        if pend:
                            pend.pop(0)()
                        for (lo_, kk) in ((0, 0), (2, 2)):
                            for which, c, dstl, dbl, sl_ in gv:
                                ub = ubs[which]
                                dst = dstl[sl_]
                                op("dve", lambda e, c=c, ub=ub, dst=dst, lo_=lo_, kk=kk: e.scalar_tensor_tensor(
                                    out=dst[:], in0=Up[ub][:, lo_:lo_ + TL], scalar=cw[:, kk, c:c + 1], in1=dst[:], op0=ALU.mult, op1=ALU.add),
                                   reads=[b_Up[ub], b_g2, dbl[sl_]], writes=[dbl[sl_]])
                        def gate_mul(p_=p_, pv_=pv_, j=j):
                            op("act", lambda e: e.activation(out=sgt[p_][:], in_=cg[p_][:], func=AF.Silu),
                               reads=[b_cg[p_]], writes=[b_sgt[p_]])
                            op("pool", lambda e: e.tensor_tensor(out=actT[:, j, :], in0=sgt[p_][:], in1=cv[pv_][:], op=ALU.mult),
                               reads=[b_sgt[p_], b_cv[pv_]], writes=[b_actc[j]])
                        pend.append(gate_mul)

                def Bdn(i, g):
                    si, t, nt, g0 = t3[i]
                    s_ = gslot(i, g)
                    for half in range(2):
                        w_ = wctr[0] % NWP
                        wctr[0] += 1
                        for c in range(NFC):
                            op("pe", lambda e, c=c, half=half, w_=w_: e.matmul(
                                Wp[w_][:], lhsT=actT[:, c, g * 128:(g + 1) * 128], rhs=w_dn_bf[:, c, half * 512:(half + 1) * 512],
                                start=(c == 0), stop=(c == NFC - 1)), reads=[b_actc[c], b_w3], writes=[b_Wp[w_]])
                        op("dve", lambda e, half=half, w_=w_: e.tensor_tensor(
                            out=x1[s_][:, half * 512:(half + 1) * 512], in0=Wp[w_][:], in1=x1[s_][:, half * 512:(half + 1) * 512], op=ALU.add),
                           reads=[b_Wp[w_], b_x1[s_]], writes=[b_x1[s_]])
                    op("sp", lambda e: e.dma_start(out=y_d[g0 + g * 128:g0 + (g + 1) * 128, :], in_=x1[s_][:]),
                       reads=[b_x1[s_]], dma=b_x1[s_])

                load_x(0, 0); load_x(0, 1); load_x(1, 0); load_x(1, 1)
                load_mix(0)
                A1(0, 0); A1(0, 1)
                load_mix(1)
                A_T(0, 0); A_halo(0); A_T(0, 1)
                A1(1, 0); A1(1, 1)
                load_mix(2)
                if N3 > 1:
                    A_T(1, 0); A_halo(1)
                for i in range(N3):
                    load_x(i + 2, 0)
                    if i + 1 < N3:
                        stage_Bup(i, tail=lambda i=i: A_T(i + 1, 1))
                    else:
                        stage_Bup(i)
                    Bdn(i, 0)
                    load_x(i + 2, 1)
                    A1(i + 2, 0)
                    Bdn(i, 1)
                    if i + 2 < N3:
                        A_T(i + 2, 0)
                        A_halo(i + 2)
                    A1(i + 2, 1)
                    load_mix(i + 3)
                out_bufs.extend(b_x1)
        S.emit(nc, final_waits=out_bufs)
    return nc


_CACHE = {}
_STOP = 9
_LIMIT = None


def _get_nc(seqs):
    key = tuple(seqs)
    if key not in _CACHE:
        _CACHE[key] = build(list(seqs))
    return _CACHE[key]


def _consts():
    invf = (np.float32(10000.0) ** (-np.arange(0, 64, 2, dtype=np.float32) / np.float32(64))).astype(np.float32)
    return {"bands": bands_const(), "invf": invf}


def run_cores(x_list, weights, seqs):
    nc = _get_nc(seqs)
    cst = _consts()
    in_maps = []
    for xc in x_list:
        m = {"x": np.ascontiguousarray(xc, dtype=np.float32)}
        m.update(weights)
        m.update(cst)
        in_maps.append(m)
    res = run_bass_kernel_spmd(nc, in_maps, core_ids=list(range(len(x_list))))
    return [r["y"] for r in res.results]


def kernel(x_prompt, x_sample, norm1_g, w_in, q_norm_g, k_norm_g, lambda_q1, lambda_k1, lambda_q2,
           lambda_k2, subln_g, w_pool, pool_scale, w_out, norm2_g, w_up, conv_w, conv_b, w_down):
    f = lambda a: np.ascontiguousarray(np.asarray(a, dtype=np.float32))
    weights = {
        "norm1_g": f(norm1_g)[0], "w_in": f(w_in)[0], "q_norm_g": f(q_norm_g)[0], "k_norm_g": f(k_norm_g)[0],
        "lambda_q1": f(lambda_q1)[0], "lambda_k1": f(lambda_k1)[0], "lambda_q2": f(lambda_q2)[0],
        "lambda_k2": f(lambda_k2)[0], "subln_g": f(subln_g)[0], "w_pool": f(w_pool)[0],
        "pool_scale": f(pool_scale)[0], "w_out": f(w_out)[0], "norm2_g": f(norm2_g)[0], "w_up": f(w_up)[0],
        "conv_w": f(conv_w)[0], "conv_b": f(conv_b)[0], "w_down": f(w_down)[0],
    }
    xp = np.asarray(x_prompt, dtype=np.float32)
    xs = np.asarray(x_sample, dtype=np.float32)
    NCORE = 8
    SP, SS = xp.shape[1], xs.shape[1]
    seqs = [SP, SP, SS]
    x_list = []
    for c in range(NCORE):
        x_list.append(np.concatenate([xp[2 * c], xp[2 * c + 1], xs[c]], axis=0))
    ys = run_cores(x_list, weights, seqs)
    yp = np.empty_like(xp)
    ysm = np.empty_like(xs)
    for c in range(NCORE):
        yp[2 * c] = ys[c][0:SP]
        yp[2 * c + 1] = ys[c][SP:2 * SP]
        ysm[c] = ys[c][2 * SP:]
    return (yp, ysm)
```
